# Optimizing a Trainium2 kernel written in Bass

```python
import jax, jax.numpy as jnp
from jax import lax
import numpy as np

D_MODEL = 1024
BATCH = 8
SEQ = 2048
DEPTH = 1
DEC_BATCH = 128
DEC_SEQ = 1
PAST_LEN = 16384
PAGE_SIZE = 128

D_CONV = D_MODEL // 2
CONF_KERNEL = 31
N_HEADS = 8
HEAD_K = 128
HEAD_V = 128
QK_DIM = N_HEADS * HEAD_K
V_DIM = N_HEADS * HEAD_V
QKV_DIM = 2 * QK_DIM + V_DIM
SHORT_CONV = 4
CHUNK = 64
D_FF = 4 * D_MODEL
N_MOD = 6
EPS = 1e-6
IN_SIZES = (2 * D_CONV, QKV_DIM, V_DIM, N_HEADS, N_HEADS, D_MODEL, D_MODEL)
IN_OFFSETS = tuple(int(o) for o in np.cumsum(IN_SIZES)[:-1])
D_IN = int(sum(IN_SIZES))

kernel_name = 'hybrid_conformer_gdn_step'


def rms_norm(x, w):
    xf = x.astype(jnp.float32)
    y = xf * lax.rsqrt(jnp.mean(xf * xf, axis=-1, keepdims=True) + EPS)
    return (y * w.astype(jnp.float32)).astype(x.dtype)


def layer_norm(x, w, b):
    xf = x.astype(jnp.float32)
    mu = jnp.mean(xf, axis=-1, keepdims=True)
    xc = xf - mu
    var = jnp.mean(xc * xc, axis=-1, keepdims=True)
    y = xc * lax.rsqrt(var + EPS) * w.astype(jnp.float32) + b.astype(jnp.float32)
    return y.astype(x.dtype)


def l2norm(x):
    return x * lax.rsqrt(jnp.sum(x * x, axis=-1, keepdims=True) + EPS)


def causal_depthwise_conv(x_ext, w):
    c = x_ext.shape[-1]
    return lax.conv_general_dilated(
        x_ext, w.astype(x_ext.dtype)[:, None, :], window_strides=(1,), padding='VALID',
        dimension_numbers=('NWC', 'WIO', 'NWC'), feature_group_count=c)


def gated_delta_rule(q, k, v, g, beta, s0):
    bsz, seqlen, nh, dk = q.shape
    dv = v.shape[-1]
    f32 = jnp.float32
    q = l2norm(q.astype(f32)) * (dk ** -0.5)
    k = l2norm(k.astype(f32))
    v = v.astype(f32)
    g = g.astype(f32)
    beta = beta.astype(f32)
    csz = min(CHUNK, seqlen)
    pad = (-seqlen) % csz
    if pad:
        padf = lambda t: jnp.pad(t, [(0, 0), (0, pad)] + [(0, 0)] * (t.ndim - 2))
        q, k, v, g, beta = padf(q), padf(k), padf(v), padf(g), padf(beta)
    lp = seqlen + pad
    nc = lp // csz
    to_c = lambda t: t.reshape(bsz, nc, csz, nh, t.shape[-1]).transpose(0, 3, 1, 2, 4)
    q, k, v = to_c(q), to_c(k), to_c(v)
    g = g.reshape(bsz, nc, csz, nh).transpose(0, 3, 1, 2)
    beta = beta.reshape(bsz, nc, csz, nh).transpose(0, 3, 1, 2)
    g = jnp.cumsum(g, axis=-1)
    tri_incl = jnp.tril(jnp.ones((csz, csz), dtype=bool))
    tri_strict = jnp.tril(jnp.ones((csz, csz), dtype=bool), -1)
    decay = jnp.exp(jnp.where(tri_incl, g[..., :, None] - g[..., None, :], -jnp.inf))
    k_beta = k * beta[..., None]
    v_beta = v * beta[..., None]
    lower = jnp.where(tri_strict, jnp.einsum('bhncd,bhnsd->bhncs', k_beta, k) * decay, 0.0)
    eye = jnp.eye(csz, dtype=f32)
    t_inv = lax.linalg.triangular_solve(eye + lower, jnp.broadcast_to(eye, lower.shape),
                                        left_side=True, lower=True, unit_diagonal=True)
    value = jnp.einsum('bhncs,bhnsv->bhncv', t_inv, v_beta)
    k_cumdecay = jnp.einsum('bhncs,bhnsd->bhncd', t_inv, k_beta * jnp.exp(g)[..., None])
    attn_intra = jnp.einsum('bhncd,bhnsd->bhncs', q, k) * decay

    def step(s, inp):
        q_c, k_c, val_c, kcd_c, att_c, g_c = inp
        v_new = val_c - jnp.einsum('bhck,bhkv->bhcv', kcd_c, s)
        o_c = (jnp.einsum('bhck,bhkv->bhcv', q_c * jnp.exp(g_c)[..., None], s)
               + jnp.einsum('bhcs,bhsv->bhcv', att_c, v_new))
        g_last = g_c[..., -1]
        s = (s * jnp.exp(g_last)[..., None, None]
             + jnp.einsum('bhck,bhcv->bhkv', k_c * jnp.exp(g_last[..., None] - g_c)[..., None], v_new))
        return s, o_c

    xs = tuple(jnp.moveaxis(t, 2, 0) for t in (q, k, value, k_cumdecay, attn_intra, g))
    s_fin, o = lax.scan(step, s0.astype(f32), xs)
    o = o.transpose(1, 0, 3, 2, 4).reshape(bsz, lp, nh, dv)[:, :seqlen]
    return o, s_fin


def hybrid_layer(x, c, conf_buf, qkv_buf, s0, w_ada, b_ada, norm1_w, w_in, conf_dw_w, conf_dw_b,
                 conf_ln_w, conf_ln_b, w_conf_out, gdn_conv_w, a_log, dt_bias, gdn_norm_w,
                 w_gdn_out, w_o, norm2_w, w_ff1, w_ff2):
    bsz, seqlen, _ = x.shape
    f32 = jnp.float32
    mod = jnp.einsum('bd,de->be', jax.nn.silu(c), w_ada) + b_ada
    shift1, scale1, gate1, shift2, scale2, gate2 = [m[:, None, :] for m in jnp.split(mod, N_MOD, axis=-1)]
    h = rms_norm(x, norm1_w) * (1 + scale1) + shift1
    proj = jnp.einsum('bld,de->ble', h, w_in)
    u_glu, qkv_raw, z, b_raw, a_raw, gate_a, gate_b = jnp.split(proj, IN_OFFSETS, axis=-1)

    glu = u_glu[..., :D_CONV] * jax.nn.sigmoid(u_glu[..., D_CONV:])
    glu_ext = jnp.concatenate([conf_buf.astype(glu.dtype), glu], axis=1)
    new_conf = glu_ext[:, glu_ext.shape[1] - (CONF_KERNEL - 1):]
    a = causal_depthwise_conv(glu_ext, conf_dw_w) + conf_dw_b
    a = jax.nn.silu(layer_norm(a, conf_ln_w, conf_ln_b))
    y_a = jnp.einsum('blc,cd->bld', a, w_conf_out)

    qkv_ext = jnp.concatenate([qkv_buf.astype(qkv_raw.dtype), qkv_raw], axis=1)
    new_qkv = qkv_ext[:, qkv_ext.shape[1] - (SHORT_CONV - 1):]
    qkv = jax.nn.silu(causal_depthwise_conv(qkv_ext, gdn_conv_w))
    q, k, v = jnp.split(qkv, (QK_DIM, 2 * QK_DIM), axis=-1)
    q = q.reshape(bsz, seqlen, N_HEADS, HEAD_K)
    k = k.reshape(bsz, seqlen, N_HEADS, HEAD_K)
    v = v.reshape(bsz, seqlen, N_HEADS, HEAD_V)
    beta = jax.nn.sigmoid(b_raw.astype(f32))
    g = -jnp.exp(a_log.astype(f32)) * jax.nn.softplus(a_raw.astype(f32) + dt_bias.astype(f32))
    o, s_new = gated_delta_rule(q, k, v, g, beta, s0)
    o = rms_norm(o.astype(x.dtype), gdn_norm_w) * jax.nn.silu(z.reshape(bsz, seqlen, N_HEADS, HEAD_V))
    y_b = jnp.einsum('blhv,hvd->bld', o, w_gdn_out.reshape(N_HEADS, HEAD_V, D_MODEL))

    merged = jax.nn.sigmoid(gate_a) * y_a + jax.nn.sigmoid(gate_b) * y_b
    x = x + gate1 * jnp.einsum('bld,de->ble', merged, w_o)

    h2 = rms_norm(x, norm2_w) * (1 + scale2) + shift2
    f = jnp.square(jax.nn.relu(jnp.einsum('bld,df->blf', h2, w_ff1)))
    x = x + gate2 * jnp.einsum('blf,fd->bld', f, w_ff2)
    return x, new_conf, new_qkv, s_new.astype(x.dtype)


def _normal(k, shape, scale):
    return jax.random.normal(k, shape, jnp.float32) * scale


def setup_inputs(seed: int = 0) -> dict:
    key = jax.random.key(seed)
    ks = jax.random.split(key, 32)
    dt = jnp.exp(jax.random.uniform(ks[20], (DEPTH, N_HEADS), jnp.float32, np.log(1e-3), np.log(1e-1)))
    return {
        'x_prompt': _normal(ks[0], (BATCH, SEQ, D_MODEL), 1.0),
        'x_sample': _normal(ks[1], (DEC_BATCH, DEC_SEQ, D_MODEL), 1.0),
        'c_prompt': _normal(ks[2], (BATCH, D_MODEL), 1.0),
        'c_sample': _normal(ks[3], (DEC_BATCH, D_MODEL), 1.0),
        'state_conf_conv': _normal(ks[4], (DEPTH, DEC_BATCH, CONF_KERNEL - 1, D_CONV), 0.5),
        'state_qkv_conv': _normal(ks[5], (DEPTH, DEC_BATCH, SHORT_CONV - 1, QKV_DIM), 1.0),
        'state_delta': _normal(ks[6], (DEPTH, DEC_BATCH, N_HEADS, HEAD_K, HEAD_V), 0.1),
        'w_ada': _normal(ks[7], (DEPTH, D_MODEL, N_MOD * D_MODEL), D_MODEL ** -0.5),
        'b_ada': _normal(ks[8], (DEPTH, N_MOD * D_MODEL), 0.02),
        'norm1_w': 1.0 + _normal(ks[9], (DEPTH, D_MODEL), 0.02),
        'w_in': _normal(ks[10], (DEPTH, D_MODEL, D_IN), D_MODEL ** -0.5),
        'conf_dw_w': _normal(ks[11], (DEPTH, CONF_KERNEL, D_CONV), CONF_KERNEL ** -0.5),
        'conf_dw_b': _normal(ks[12], (DEPTH, D_CONV), 0.02),
        'conf_ln_w': 1.0 + _normal(ks[13], (DEPTH, D_CONV), 0.02),
        'conf_ln_b': _normal(ks[14], (DEPTH, D_CONV), 0.02),
        'w_conf_out': _normal(ks[15], (DEPTH, D_CONV, D_MODEL), D_CONV ** -0.5),
        'gdn_conv_w': _normal(ks[16], (DEPTH, SHORT_CONV, QKV_DIM), SHORT_CONV ** -0.5),
        'a_log': jnp.log(jax.random.uniform(ks[17], (DEPTH, N_HEADS), jnp.float32, 1.0, 16.0)),
        'dt_bias': dt + jnp.log(-jnp.expm1(-dt)),
        'gdn_norm_w': 1.0 + _normal(ks[18], (DEPTH, HEAD_V), 0.02),
        'w_gdn_out': _normal(ks[19], (DEPTH, V_DIM, D_MODEL), V_DIM ** -0.5),
        'w_o': _normal(ks[21], (DEPTH, D_MODEL, D_MODEL), D_MODEL ** -0.5),
        'norm2_w': 1.0 + _normal(ks[22], (DEPTH, D_MODEL), 0.02),
        'w_ff1': _normal(ks[23], (DEPTH, D_MODEL, D_FF), D_MODEL ** -0.5),
        'w_ff2': _normal(ks[24], (DEPTH, D_FF, D_MODEL), D_FF ** -0.5),
        'final_norm_w': 1.0 + _normal(ks[25], (D_MODEL,), 0.02),
    }


def reference(x_prompt, x_sample, c_prompt, c_sample, state_conf_conv, state_qkv_conv, state_delta,
              w_ada, b_ada, norm1_w, w_in, conf_dw_w, conf_dw_b, conf_ln_w, conf_ln_b, w_conf_out,
              gdn_conv_w, a_log, dt_bias, gdn_norm_w, w_gdn_out, w_o, norm2_w, w_ff1, w_ff2,
              final_norm_w):
    bp = x_prompt.shape[0]
    hp, hs = x_prompt, x_sample
    conf_p, qkv_p, delta_p = [], [], []
    conf_s, qkv_s, delta_s = [], [], []
    for l in range(DEPTH):
        lw = (w_ada[l], b_ada[l], norm1_w[l], w_in[l], conf_dw_w[l], conf_dw_b[l], conf_ln_w[l],
              conf_ln_b[l], w_conf_out[l], gdn_conv_w[l], a_log[l], dt_bias[l], gdn_norm_w[l],
              w_gdn_out[l], w_o[l], norm2_w[l], w_ff1[l], w_ff2[l])
        conf0 = jnp.zeros((bp, CONF_KERNEL - 1, D_CONV), x_prompt.dtype)
        qkv0 = jnp.zeros((bp, SHORT_CONV - 1, QKV_DIM), x_prompt.dtype)
        s00 = jnp.zeros((bp, N_HEADS, HEAD_K, HEAD_V), jnp.float32)
        hp, cp, qp, sp = hybrid_layer(hp, c_prompt, conf0, qkv0, s00, *lw)
        hs, cs, qs, ss = hybrid_layer(hs, c_sample, state_conf_conv[l], state_qkv_conv[l], state_delta[l], *lw)
        conf_p.append(cp); qkv_p.append(qp); delta_p.append(sp)
        conf_s.append(cs); qkv_s.append(qs); delta_s.append(ss)
    y_prompt = rms_norm(hp, final_norm_w)
    y_sample = rms_norm(hs, final_norm_w)
    return (y_prompt, y_sample, jnp.stack(conf_p), jnp.stack(qkv_p), jnp.stack(delta_p),
            jnp.stack(conf_s), jnp.stack(qkv_s), jnp.stack(delta_s))
```

```python
import numpy as np
import concourse.bass as bass
import concourse.mybir as mybir
from concourse.bass_utils import run_bass_kernel_spmd

F32 = mybir.dt.float32
F32R = mybir.dt.float32r
AF = mybir.ActivationFunctionType
ALU = mybir.AluOpType

SAME_ENGINE_SYNC = True


class Buf:
    def __init__(self, name, ap, root=None, excl=False):
        self.name = name
        self.ap = ap
        self.last_w = None
        self.readers = []
        self.root = root if root is not None else self
        self.excl = excl

    def __getitem__(self, idx):
        return self.ap[idx]

    @property
    def r(self):
        return self.ap.bitcast(F32R)


class Op:
    __slots__ = ("eng", "fn", "waits", "needed", "count", "is_dma", "sem", "semval", "idx")

    def __init__(self, eng, fn):
        self.eng = eng
        self.fn = fn
        self.waits = []
        self.needed = False
        self.count = 0
        self.is_dma = False
        self.sem = None
        self.semval = 0


class Sched:
    ENGS = ("pe", "act", "dve", "pool", "sp")

    def __init__(self, nc, n_dma_sems=(("sp", 28), ("pool", 16), ("act", 8))):
        self.nc = nc
        self.ops = {e: [] for e in self.ENGS}
        self.ctx = []
        self.esem = {}
        for e in self.ENGS:
            self.esem[e] = nc.alloc_semaphore(name=f"es_{e}")
        self.dsem = {}
        for q, n in n_dma_sems:
            self.dsem[q] = [[nc.alloc_semaphore(name=f"ds_{q}{i}"), 0, None] for i in range(n)]
        self.dcnt = {q: 0 for q, _ in n_dma_sems}
        self.out_dmas = []
        self.nbuf = 0

    def sb(self, name, shape, dtype=F32):
        self.nbuf += 1
        t = self.nc.alloc_sbuf_tensor(f"{name}_{self.nbuf}", list(shape), dtype)
        return Buf(name, t[:])

    def ps(self, name, shape, dtype=F32):
        self.nbuf += 1
        t = self.nc.alloc_psum_tensor(f"{name}_{self.nbuf}", list(shape), dtype)
        return Buf(name, t[:], excl=True)

    def _add(self, eng, fn, reads, writes):
        op = Op(eng, fn)
        deps = []
        rr, ww = [], []
        for w in writes:
            if w.root not in ww:
                ww.append(w.root)
        for r in reads:
            r = r.root
            if r.excl:
                if r not in ww:
                    ww.append(r)
            elif r not in rr:
                rr.append(r)
        reads, writes = rr, ww
        for r in reads:
            if r.last_w is not None:
                deps.append(r.last_w)
        for w in writes:
            if w.last_w is not None:
                deps.append(w.last_w)
            deps.extend(w.readers)
        seen = set()
        latest = {}
        for d in deps:
            if id(d) in seen or d is op:
                continue
            seen.add(id(d))
            if d.is_dma:
                op.waits.append(d)
                continue
            if d.eng == eng and (eng == "pe" or not SAME_ENGINE_SYNC):
                continue
            cur = latest.get(d.eng)
            if cur is None or d.idx > cur.idx:
                latest[d.eng] = d
        for d in latest.values():
            d.needed = True
            op.waits.append(d)
        for w in writes:
            w.last_w = op
            w.readers = []
        for r in reads:
            if r not in writes:
                r.readers.append(op)
        op.idx = len(self.ops[eng])
        self.ops[eng].append(op)
        return op

    def mm(self, out, lhsT, rhs, start, stop, reads, writes):
        return self._add("pe", lambda e: e.matmul(out, lhsT, rhs, start=start, stop=stop), reads, writes)

    def tr(self, out, in_, ident, reads, writes):
        return self._add("pe", lambda e: e.transpose(out, in_, ident), reads, writes)

    def act(self, out, in_, func, scale=1.0, bias=0.0, reads=(), writes=()):
        return self._add("act", lambda e: e.activation(out=out, in_=in_, func=func, bias=bias, scale=scale),
                         reads, writes)

    def dve(self, fn, reads, writes):
        return self._add("dve", fn, reads, writes)

    def pool(self, fn, reads, writes):
        return self._add("pool", fn, reads, writes)

    def any(self, eng, fn, reads, writes):
        return self._add(eng, fn, reads, writes)

    def dma(self, q, out, in_, reads, writes, out_final=False, precook=True):
        nc = self.nc

        def f(e):
            nc.dge_precook = precook
            ins = e.dma_start(out=out, in_=in_)
            nc.dge_precook = True
            return ins
        op = self._add(q, f, reads, writes)
        op.is_dma = True
        slots = self.dsem[q]
        k = self.dcnt[q] % len(slots)
        self.dcnt[q] += 1
        slot = slots[k]
        if slot[2] is not None:
            op.waits.append(slot[2])
        slot[1] += 16
        op.sem = slot[0]
        op.semval = slot[1]
        slot[2] = op
        if out_final:
            self.out_dmas.append(op)
        return op

    def make_ident(self, buf, n=128):
        self._add("pool", lambda e: e.memset(buf.ap, 0.0), [], [buf])
        return self._add("pool", lambda e: e.affine_select(
            out=buf.ap, in_=buf.ap, pattern=[[-1, n]], compare_op=ALU.not_equal,
            fill=1.0, base=0, channel_multiplier=1), [buf], [buf])

    def emit(self):
        nc = self.nc
        for e in self.ENGS:
            c = 0
            for op in self.ops[e]:
                if op.is_dma:
                    continue
                if op.needed:
                    c += 1
                op.count = c
        engmap = {"pe": "tensor", "act": "scalar", "dve": "vector", "pool": "gpsimd", "sp": "sync"}
        last_eng = "sp"

        def run(ename):
            def body(eng):
                waited = {}
                for op in self.ops[ename]:
                    for d in op.waits:
                        if d.is_dma:
                            sem, val = d.sem, d.semval
                        else:
                            sem, val = self.esem[d.eng], d.count
                        key = id(sem)
                        if waited.get(key, -1) >= val:
                            continue
                        waited[key] = val
                        eng.wait_ge(sem, val)
                    ins = op.fn(eng)
                    if op.is_dma:
                        ins.then_inc(op.sem, 16)
                    elif op.needed:
                        ins.then_inc(self.esem[ename], 1)
                if ename == last_eng:
                    for d in self.out_dmas:
                        eng.wait_ge(d.sem, d.semval)
            return body

        with nc.Block() as block:
            for ename in self.ENGS:
                getattr(block, engmap[ename])(run(ename))


class FreeList:
    def __init__(self, items):
        self.free = list(items)

    def get(self):
        if not self.free:
            raise RuntimeError("pool empty")
        return self.free.pop(0)

    def put(self, *items):
        for it in items:
            self.free.append(it)


D = 1024
DC = 512
NH = 8
TB = 512
NB = 4
NS = 16
L_SEQ = 2048
DIN = 7184
OFF_Q, OFF_K, OFF_V, OFF_Z, OFF_B, OFF_A, OFF_GA, OFF_GB = 1024, 2048, 3072, 4096, 5120, 5128, 5136, 6160
EPS = 1e-6
NMOD = 48
BIG = 30000.0
MUL, ADD, SUB, MAX = ALU.mult, ALU.add, ALU.subtract, ALU.max


class _Stop(Exception):
    pass


def build_program(n_blocks=NB, with_samples=True, stop=None):
    nc = bass.Bass("TRN2", target_bir_lowering=False)
    S = Sched(nc)
    try:
        _build_body(nc, S, n_blocks, with_samples, stop)
    except _Stop:
        pass
    S.emit()
    return nc


def _build_body(nc, S, n_blocks, with_samples, stop):
    dbg_out = nc.dram_tensor("dbg", [128, 512], F32, kind="ExternalOutput").ap() if stop else None

    def chk(name, ap, buf, rows=128, cols=512):
        if stop == name:
            import os as _os
            nsp = int(_os.environ.get("SPINCHK", "0"))
            if nsp:
                spb = S.sb("spinbuf", [128, 512])
                S.dve(lambda e: e.memset(spb.ap, 1.0), [], [spb])
                for _ in range(nsp):
                    S.dve(lambda e: e.tensor_scalar(out=spb.ap, in0=spb.ap, scalar1=1.0, scalar2=None, op0=MUL),
                          [spb, buf], [spb, buf])
            S.dma("pool", dbg_out[0:rows, 0:cols], ap, [buf], [], out_final=True)
            raise _Stop()

    def din(name, shape):
        return nc.dram_tensor(name, list(shape), F32, kind="ExternalInput").ap()

    def dout(name, shape):
        return nc.dram_tensor(name, list(shape), F32, kind="ExternalOutput").ap()

    x_p = din("x_p", [L_SEQ, D]); x_s = din("x_s", [NS, D]); c_all = din("c_all", [NS + 1, D])
    st_conf = din("st_conf", [NS, 30, DC]); st_qkv = din("st_qkv", [NS, 3, 3072])
    st_delta = din("st_delta", [NS, NH, 128, 128])
    w_ada = din("w_ada", [D, 6 * D + 128]); b_ada = din("b_ada", [6 * D]); norm1_w = din("norm1_w", [D])
    w_in = din("w_in", [D, DIN + 128]); conf_dw_w = din("conf_dw_w", [31, DC]); conf_dw_b = din("conf_dw_b", [DC])
    conf_ln_w = din("conf_ln_w", [DC]); conf_ln_b = din("conf_ln_b", [DC]); w_conf_out = din("w_conf_out", [DC, D + 128])
    gdn_conv_w = din("gdn_conv_w", [4, 3072]); a_log = din("a_log", [NH]); dt_bias = din("dt_bias", [NH])
    gdn_norm_w = din("gdn_norm_w", [128]); w_gdn_out = din("w_gdn_out", [D, D + 128]); w_o = din("w_o", [D, D + 128])
    norm2_w = din("norm2_w", [D]); w_ff1 = din("w_ff1", [D, 4 * D + 128]); w_ff2 = din("w_ff2", [4 * D, D + 128])
    final_norm_w = din("final_norm_w", [D])
    y_p = dout("y_p", [L_SEQ, D]); y_s = dout("y_s", [NS, D]); nconf_p = dout("nconf_p", [30, DC])
    nqkv_p = dout("nqkv_p", [3, 3072]); ndelta_p = dout("ndelta_p", [NH, 128, 128])
    nconf_s = dout("nconf_s", [NS, 30, DC]); nqkv_s = dout("nqkv_s", [NS, 3, 3072])
    ndelta_s = dout("ndelta_s", [NS, NH, 128, 128])

    def kview(w):
        return w.rearrange("(kc p) e -> p kc e", p=128)
    w_ada_v, w_in_v, w_co_v, w_go_v, w_o_v, w_f1_v, w_f2_v = (kview(w) for w in
                                                              (w_ada, w_in, w_conf_out, w_gdn_out, w_o, w_ff1, w_ff2))

    NW = 5
    wpool = FreeList([S.sb(f"w{i}", [128, 8, 128]) for i in range(NW)])
    NR, NP = 28, 20
    rpool = FreeList([S.sb(f"ru{i}", [128, 512]) for i in range(NR)])
    ppool = FreeList([S.sb(f"pu{i}", [128, 512]) for i in range(NP)])
    banks = [S.ps(f"bank{i}", [128, 512]) for i in range(8)]
    class _BankPool:
        def __init__(self, bs):
            self.free = list(bs)

        def get(self, w):
            if not self.free:
                raise RuntimeError("psum pool empty")
            bk = self.free.pop(0)
            v = Buf("pv", bk.ap[:, 0:w], root=bk)
            v.bank = bk
            return v

        def put(self, *vs):
            for v in vs:
                self.free.append(v.bank)

    class _Shim:
        def __init__(self, pool, w):
            self.pool = pool; self.w = w

        def get(self):
            return self.pool.get(self.w)

        def put(self, *vs):
            self.pool.put(*vs)
    _bp = _BankPool(banks)
    pfull = _Shim(_bp, 512); phalf = _Shim(_bp, 256); pq = _Shim(_bp, 128)
    NT_R, NT_P = 26, 19
    trpool = FreeList([S.sb(f"tr{i}", [128, 128]) for i in range(NT_R)])
    tppool = FreeList([S.sb(f"tp{i}", [128, 128]) for i in range(NT_P)])
    r2pool = FreeList([S.sb(f"r2{i}", [128, 256]) for i in range(16)])

    def load_w(src, nk=8, ncol=128):
        b = wpool.get()
        S.dma("sp", b.ap[:, 0:nk, 0:ncol].bitcast(F32R), src.bitcast(F32R), [], [b], precook=False)
        return b

    ident = S.sb("ident", [128, 128]); S.make_ident(ident)
    ones = S.sb("ones", [128, 128]); S.pool(lambda e: e.memset(ones.ap, 1.0), [], [ones])
    epsb = S.sb("epsb", [128, 1]); S.pool(lambda e: e.memset(epsb.ap, EPS), [], [epsb])
    chk("ident", ident.ap, ident, 128, 128)

    def mk_mask(name, wt, keep, fill, sign, cmp, fr, fc, fv):
        b = S.sb(name, [128, wt * 128])
        S.pool(lambda e: e.memset(b.ap, keep), [], [b])
        if wt > 1:
            v = b.ap.rearrange("p (u j) -> p u j", j=128); pat = [[0, wt], [-sign, 128]]
        else:
            v = b.ap; pat = [[-sign, 128]]
        S.pool(lambda e: e.affine_select(out=v, in_=v, pattern=pat, compare_op=cmp, fill=fill, base=0,
                                         channel_multiplier=sign), [b], [b])
        for u in range(wt):
            S.pool(lambda e, u=u: e.memset(b.ap[fr[0]:fr[1], u * 128 + fc[0]:u * 128 + fc[1]], fv), [b], [b])
        return b
    mL4 = mk_mask("mL4", 4, 0.0, BIG, 1, ALU.is_gt, (64, 128), (0, 64), BIG)
    mU4 = mk_mask("mU4", 4, 0.0, BIG, -1, ALU.is_ge, (0, 64), (64, 128), BIG)
    Mcum = mk_mask("Mcum", 1, 1.0, 0.0, -1, ALU.is_ge, (0, 64), (64, 128), 0.0)
    Maft = mk_mask("Maft", 1, 1.0, 0.0, 1, ALU.is_gt, (64, 128), (0, 64), 0.0)
    M32 = S.sb("M32", [128, 128])
    S.pool(lambda e: e.memset(M32.ap, 0.0), [], [M32])
    for q_ in range(4):
        S.pool(lambda e, q_=q_: e.memset(M32.ap[32 * q_:32 * q_ + 32, 32 * q_:32 * q_ + 32], 1.0), [M32], [M32])
    stage = S.sb("stage", [128, 128])
    S.dve(lambda e: e.memset(stage.ap, 0.0), [], [stage])
    r0 = 0
    VOFF = {}
    for nm, v, n in (("b_ada", b_ada, 48), ("n1", norm1_w, 8), ("n2", norm2_w, 8), ("nf", final_norm_w, 8),
                     ("dwb", conf_dw_b, 4), ("lnw", conf_ln_w, 4), ("lnb", conf_ln_b, 4), ("gnw", gdn_norm_w, 1)):
        S.dma("pool", stage.ap[r0:r0 + n, :], v.rearrange("(r p) -> r p", p=128), [], [stage])
        VOFF[nm] = r0
        r0 += n
    vecT = S.sb("vecT", [128, 96])
    t_ = pq.get()
    S.tr(t_.ap, stage.ap, ident.ap, [stage, ident], [t_])
    S.act(vecT.ap[:, 0:85], t_.ap[:, 0:85], AF.Copy, reads=[t_], writes=[vecT])
    pq.put(t_)
    S.dve(lambda e: e.tensor_scalar(out=vecT.ap[:, 85:93], in0=vecT.ap[:, VOFF["lnw"]:VOFF["lnw"] + 8], scalar1=0.5,
                                    scalar2=None, op0=MUL), [vecT], [vecT])
    chk("vecT", vecT.ap, vecT, 128, 96)

    def vcol(nm, i=0, off=0):
        c = VOFF[nm] + i + off
        return vecT.ap[:, c:c + 1]

    stg = tppool.get()
    S.dve(lambda e, stg=stg: e.memset(stg.ap, 0.0), [], [stg])
    S.dma("pool", stg.ap[0:124, :], conf_dw_w.rearrange("j (c p) -> (j c) p", p=128), [], [stg])
    dwT = S.sb("dwT", [128, 31, 4])
    t_ = pq.get()
    S.tr(t_.ap, stg.ap, ident.ap, [stg, ident], [t_])
    S.act(dwT.ap.rearrange("p j c -> p (j c)"), t_.ap[:, 0:124], AF.Copy, reads=[t_], writes=[dwT])
    pq.put(t_); tppool.put(stg)
    chk("dwT", dwT.ap.rearrange("p j c -> p (j c)"), dwT, 128, 124)
    stg = tppool.get()
    S.dve(lambda e, stg=stg: e.memset(stg.ap, 0.0), [], [stg])
    S.dma("pool", stg.ap[0:96, :], gdn_conv_w.rearrange("j (c p) -> (j c) p", p=128), [], [stg])
    gcT = S.sb("gcT", [128, 4, 24])
    t_ = pq.get()
    S.tr(t_.ap, stg.ap, ident.ap, [stg, ident], [t_])
    S.act(gcT.ap.rearrange("p j c -> p (j c)"), t_.ap[:, 0:96], AF.Copy, reads=[t_], writes=[gcT])
    pq.put(t_); tppool.put(stg)
    chk("gcT", gcT.ap.rearrange("p j c -> p (j c)"), gcT, 128, 96)
    hp8 = S.sb("hp8", [8, 4])
    S.dma("pool", hp8.ap[:, 0:1], a_log.rearrange("(h o) -> h o", o=1), [], [hp8])
    S.dma("pool", hp8.ap[:, 1:2], dt_bias.rearrange("(h o) -> h o", o=1), [], [hp8])
    S.act(hp8.ap[:, 2:3], hp8.ap[:, 0:1], AF.Exp, reads=[hp8], writes=[hp8])
    S.dve(lambda e: e.tensor_scalar(out=hp8.ap[:, 2:3], in0=hp8.ap[:, 2:3], scalar1=-1.0, scalar2=None, op0=MUL),
          [hp8], [hp8])
    hpb = S.sb("hpb", [128, 3, 8])
    S.dma("pool", hpb.ap[:, 0, :], a_log.partition_broadcast(128), [], [hpb])
    S.dma("pool", hpb.ap[:, 1, :], dt_bias.partition_broadcast(128), [], [hpb])
    S.act(hpb.ap[:, 2, :], hpb.ap[:, 0, :], AF.Exp, reads=[hpb], writes=[hpb])
    S.dve(lambda e: e.tensor_scalar(out=hpb.ap[:, 2, :], in0=hpb.ap[:, 2, :], scalar1=-1.0, scalar2=None, op0=MUL),
          [hpb], [hpb])

    chk("hpb", hpb.ap.rearrange("p a b -> p (a b)"), hpb, 128, 24)
    chk("hp8", hp8.ap, hp8, 8, 4)
    def tt(out, a, b, op, reads, writes, eng="dve"):
        return S.any(eng, lambda e: e.tensor_tensor(out=out, in0=a, in1=b, op=op), reads, writes)

    def ts(out, a, s1, op0, reads, writes, s2=None, op1=None, eng="dve"):
        if op1 is None:
            return S.any(eng, lambda e: e.tensor_scalar(out=out, in0=a, scalar1=s1, scalar2=None, op0=op0),
                         reads, writes)
        return S.any(eng, lambda e: e.tensor_scalar(out=out, in0=a, scalar1=s1, scalar2=s2, op0=op0, op1=op1),
                     reads, writes)

    def stt(out, a, sc, b, op0, op1, reads, writes):
        return S.dve(lambda e: e.scalar_tensor_tensor(out=out, in0=a, scalar=sc, in1=b, op0=op0, op1=op1),
                     reads, writes)

    def silu_from_psum(pb, out_ap, out_buf, n=512, pre_scale=1.0):
        xh = ppool.get(); th = ppool.get()
        S.act(xh.ap[:, 0:n], pb.ap[:, 0:n], AF.Identity, scale=0.5 * pre_scale, reads=[pb], writes=[xh])
        S.act(th.ap[:, 0:n], xh.ap[:, 0:n], AF.Tanh, reads=[xh], writes=[th])
        stt(out_ap, th.ap[:, 0:n], 1.0, xh.ap[:, 0:n], ADD, MUL, [th, xh], [out_buf])
        ppool.put(xh, th)

    def rstd_from_chunks(chunks, denom, eps=EPS):
        pb = pfull.get()
        n = len(chunks)
        for i, (ap, buf) in enumerate(chunks):
            sq = rpool.get()
            tt(sq.r, ap, ap, MUL, [buf], [sq], eng="pool")
            S.mm(pb.ap, ones.r, sq.r, i == 0, i == n - 1, [ones, sq], [pb])
            rpool.put(sq)
        ln = ppool.get()
        S.act(ln.ap, pb.ap, AF.Ln, scale=1.0 / denom, bias=epsb.ap, reads=[pb, epsb], writes=[ln])
        pfull.put(pb)
        S.act(ln.ap, ln.ap, AF.Exp, scale=-0.5, reads=[ln], writes=[ln])
        return ln

    NC18 = 18
    c0 = ppool.get(); c1 = ppool.get()
    S.dve(lambda e: e.memset(c0.ap, 0.0), [], [c0])
    S.dve(lambda e: e.memset(c1.ap, 0.0), [], [c1])
    S.dma("sp", c0.ap[0:NS + 1, :], c_all[:, 0:512], [], [c0])
    S.dma("sp", c1.ap[0:NS + 1, :], c_all[:, 512:1024], [], [c1])
    scT = S.sb("scT", [128, 8, NC18]); sch = S.sb("sch", [128, 8, NC18]); sct = S.sb("sct", [128, 8, NC18])
    S.dve(lambda e: e.memset(sch.ap.rearrange("p a b -> p (a b)"), 0.0), [], [sch])
    for kc in range(8):
        src = c0 if kc < 4 else c1
        t_ = pq.get()
        S.tr(t_.ap, src.ap[:, (kc % 4) * 128:(kc % 4 + 1) * 128], ident.ap, [src, ident], [t_])
        S.act(sch.ap[:, kc, 0:NS + 1], t_.ap[:, 0:NS + 1], AF.Identity, scale=0.5, reads=[t_], writes=[sch])
        pq.put(t_)
    ppool.put(c0, c1)
    chk("sch", sch.ap.rearrange("p a b -> p (a b)"), sch, 128, 8 * NC18)
    S.act(sct.ap, sch.ap, AF.Tanh, reads=[sch], writes=[sct])
    chk("sct", sct.ap.rearrange("p a b -> p (a b)"), sct, 128, 8 * NC18)
    stt(scT.r, sct.ap, 1.0, sch.ap, ADD, MUL, [sct, sch], [scT])
    chk("scT", scT.ap.rearrange("p a b -> p (a b)"), scT, 128, 8 * NC18)
    modT = S.sb("modT", [128, NC18, 48])
    modS = S.sb("modS", [128, 48, NS])
    def mod_tile(e_):
        wt = load_w(w_ada_v[:, :, e_ * 128:(e_ + 1) * 128])
        t_ = pq.get()
        for kc in range(8):
            S.mm(t_.ap[:, 0:NC18], wt.r[:, kc, :], scT.r[:, kc, :], kc == 0, kc == 7, [wt, scT], [t_])
        S.act(modT.ap[:, :, e_], t_.ap[:, 0:NC18], AF.Identity, bias=vcol("b_ada", e_), reads=[t_, vecT], writes=[modT])
        S.act(modS.ap[:, e_, :], t_.ap[:, 1:NS + 1], AF.Identity, bias=vcol("b_ada", e_), reads=[t_, vecT], writes=[modS])
        pq.put(t_)
        wpool.put(wt)
    for e_ in range(16):
        mod_tile(e_)
    dp = S.sb("dp", [128, 24])
    modP = S.sb("modP", [128, 48])
    S.dve(lambda e: e.tensor_copy(out=modP.ap[:, 0:16], in_=modT.ap[:, 0, 0:16]), [modT], [modP])
    stt(dp.ap[:, 0:8], modP.ap[:, 8:16], 1.0, vecT.ap[:, VOFF["n1"]:VOFF["n1"] + 8], ADD, MUL, [modP, vecT], [dp])

    def modB_gen():
        for e_ in range(16, NMOD):
            mod_tile(e_)
            yield
    modB = modB_gen()

    def finish_modB():
        for _ in modB:
            pass
        S.dve(lambda e: e.tensor_copy(out=modP.ap[:, 16:48], in_=modT.ap[:, 0, 16:48]), [modT], [modP])
        ts(dp.ap[:, 8:16], modP.ap[:, 16:24], 0.5, MUL, [modP], [dp])
        stt(dp.ap[:, 16:24], modP.ap[:, 32:40], 1.0, vecT.ap[:, VOFF["n2"]:VOFF["n2"] + 8], ADD, MUL, [modP, vecT], [dp])
    A1p = lambda c: dp.ap[:, c:c + 1]
    B1p = lambda c: modP.ap[:, c:c + 1]
    HG1p = lambda c: dp.ap[:, 8 + c:9 + c]
    A2p = lambda c: dp.ap[:, 16 + c:17 + c]
    B2p = lambda c: modP.ap[:, 24 + c:25 + c]
    G2p = lambda c: modP.ap[:, 40 + c:41 + c]
    chk("modT", modT.ap.rearrange("p a b -> p (a b)")[:, 0:512], modT, 128, 512)

    chk("dp", dp.ap, dp, 128, 24)
    gluext = S.sb("gluext", [128, 4, 30 + TB])
    for c in range(4):
        S.dve(lambda e, c=c: e.memset(gluext.ap[:, c, 0:30], 0.0), [], [gluext])
    chk("init0", gluext.ap[:, 0, 0:128], gluext, 128, 128)
    qkvext = [S.sb(f"qkvext{i}", [128, 3, 3 + TB]) for i in range(1)]
    qkvhist = S.sb("qkvhist", [128, 24, 3])
    S.dve(lambda e: e.memset(qkvhist.ap, 0.0), [], [qkvhist])
    chk("init1", qkvhist.ap.rearrange("p a b -> p (a b)"), qkvhist, 128, 72)
    Sst = [S.sb(f"Sst{h}", [128, 128]) for h in range(NH)]
    for h in range(NH):
        S.dve(lambda e, h=h: e.memset(Sst[h].ap, 0.0), [], [Sst[h]])
    chk("init2", Sst[7].ap, Sst[7], 128, 128)
    vnew = S.sb("vnew", [128, 128])
    S.dve(lambda e: e.memset(vnew.ap, 0.0), [], [vnew])
    KdA = [S.sb(f"KdA{i}", [128, 128]) for i in range(4)]
    KdB = [S.sb(f"KdB{i}", [128, 128]) for i in range(4)]
    for i in range(4):
        S.dve(lambda e, i=i: e.memset(KdA[i].ap, 0.0), [], [KdA[i]])
        S.dve(lambda e, i=i: e.memset(KdB[i].ap, 0.0), [], [KdB[i]])
    colT = [S.sb(f"colT{u}", [128, 48]) for u in range(4)]

    chk("init", vnew.ap, vnew, 128, 128)
    def samp_mm(wt, nk, ncol, samp, c0=0):
        srhs, dst_ap, dst_buf, mode = samp
        t_ = pq.get()
        for kc in range(nk):
            S.mm(t_.ap[0:ncol, 0:NS], wt.r[:, kc, c0:c0 + ncol], srhs.r[:, kc, :], kc == 0, kc == nk - 1, [wt, srhs], [t_])
        if mode == "copy":
            S.act(dst_ap, t_.ap[0:ncol, 0:NS], AF.Copy, reads=[t_], writes=[dst_buf])
        elif mode == "tanh":
            S.act(dst_ap, t_.ap[0:ncol, 0:NS], AF.Tanh, scale=0.5, reads=[t_], writes=[dst_buf])
        elif mode == "relu":
            S.act(dst_ap, t_.ap[0:ncol, 0:NS], AF.Relu, reads=[t_], writes=[dst_buf])
        pq.put(t_)

    def proj(wcols, rhs_bufs, out_ap, out_buf, nk=8, wview=w_in_v, ncol=128, rsl=slice(0, TB), samp=None):
        wt = load_w(wview[:, 0:nk, wcols:wcols + ncol], nk=nk, ncol=ncol)
        for kc in range(nk):
            S.mm(out_ap, wt.r[:, kc, 0:ncol], rhs_bufs[kc].r[:, rsl], kc == 0, kc == nk - 1, [wt, rhs_bufs[kc]], [out_buf])
        if samp is not None:
            samp_mm(wt, nk, ncol, samp)
        wpool.put(wt)

    xsT = S.sb("xsT", [128, 8, NS]); hsT = S.sb("hsT", [128, 8, NS]); sA = S.sb("sA", [128, 8, NS])
    a1s = S.sb("a1s", [128, 4, NS]); aTs = S.sb("aTs", [128, 4, NS]); gS = S.sb("gS", [128, 8, NS])
    mgS = S.sb("mgS", [128, 8, NS]); qkvS = S.sb("qkvS", [128, 24, NS]); qkvC = S.sb("qkvC", [128, 24, NS])
    zS = S.sb("zS", [128, 8, NS]); oTs = S.sb("oTs", [128, 8, NS]); fS = S.sb("fS", [128, 32, NS])
    rawS = S.sb("rawS", [128, 8, NS]); ba8 = S.sb("ba8", [8, 2, NS])

    def flat(buf, a=None, b=None):
        v = buf.ap if a is None else buf.ap[:, a:b, :]
        return v.rearrange("p a s -> p (a s)")

    def s_rstd(src, nch, denom):
        sq = tppool.get()
        n = nch * NS
        tt(sq.ap[:, 0:n], flat(src, 0, nch), flat(src, 0, nch), MUL, [src], [sq])
        t_ = pq.get()
        for kc in range(nch):
            S.mm(t_.ap[:, 0:NS], ones.ap, sq.ap[:, kc * NS:(kc + 1) * NS], kc == 0, kc == nch - 1, [ones, sq], [t_])
        S.act(sq.ap[:, 0:NS], t_.ap[:, 0:NS], AF.Ln, scale=1.0 / denom, bias=epsb.ap, reads=[t_, epsb], writes=[sq])
        pq.put(t_)
        S.act(sq.ap[:, 0:NS], sq.ap[:, 0:NS], AF.Exp, scale=-0.5, reads=[sq], writes=[sq])
        return sq

    def s_modnorm(dst, vec_nm, sc_off, sh_off):
        rs_ = s_rstd(xsT, 8, float(D))
        t1 = tppool.get(); t2 = tppool.get()
        for kc in range(8):
            ts(t1.ap[:, 0:NS], modS.ap[:, sc_off + kc, :], 1.0, ADD, [modS], [t1])
            ts(t1.ap[:, 0:NS], t1.ap[:, 0:NS], vcol(vec_nm, kc), MUL, [t1, vecT], [t1])
            tt(t2.ap[:, 0:NS], xsT.ap[:, kc, :], rs_.ap[:, 0:NS], MUL, [xsT, rs_], [t2])
            tt(t2.ap[:, 0:NS], t2.ap[:, 0:NS], t1.ap[:, 0:NS], MUL, [t2, t1], [t2])
            tt(dst.r[:, kc, :], t2.ap[:, 0:NS], modS.ap[:, sh_off + kc, :], ADD, [t2, modS], [dst])
        tppool.put(rs_, t1, t2)

    def s_to_tm(src_fn, nch, dram_fn, reads):
        for g0 in range(0, nch, 4):
            pb = pfull.get()
            ng = min(4, nch - g0)
            for cc in range(ng):
                S.tr(pb.ap[0:NS, cc * 128:(cc + 1) * 128], src_fn(g0 + cc), ident.ap, reads + [ident], [pb])
            o_ = ppool.get()
            S.act(o_.ap[0:NS, 0:ng * 128], pb.ap[0:NS, 0:ng * 128], AF.Copy, reads=[pb], writes=[o_])
            pfull.put(pb)
            S.dma("pool", dram_fn(g0, ng), o_.ap[0:NS, 0:ng * 128], [o_], [], out_final=True)
            ppool.put(o_)

    for b in range(n_blocks):
        t0 = b * TB
        last = (b == NB - 1)
        xtm = [[ppool.get(), ppool.get()] for _ in range(4)]
        for s_ in range(4):
            for hf in range(2):
                S.dma("pool", xtm[s_][hf].ap, x_p[t0 + s_ * 128:t0 + (s_ + 1) * 128, hf * 512:(hf + 1) * 512],
                      [], [xtm[s_][hf]])
        xT = [ppool.get() for _ in range(8)]
        for c in range(8):
            pb = pfull.get()
            for s_ in range(4):
                src = xtm[s_][c // 4]
                S.tr(pb.ap[:, s_ * 128:(s_ + 1) * 128], src.ap[:, (c % 4) * 128:(c % 4 + 1) * 128], ident.ap,
                     [src, ident], [pb])
            if c % 2 == 0:
                S.act(xT[c].ap, pb.ap, AF.Copy, reads=[pb], writes=[xT[c]])
            else:
                S.dve(lambda e, c=c, pb=pb, xT=xT: e.tensor_copy(out=xT[c].ap, in_=pb.ap), [pb], [xT[c]])
            pfull.put(pb)
        for s_ in range(4):
            ppool.put(*xtm[s_])
        chk("xT", xT[1].ap, xT[1])
        rs = rstd_from_chunks([(xT[c].ap, xT[c]) for c in range(8)], float(D))
        chk("rs", rs.ap, rs)
        hT = [rpool.get() for _ in range(8)]
        for c in range(8):
            tt(hT[c].r, xT[c].ap, rs.ap, MUL, [xT[c], rs], [hT[c]])
            ts(hT[c].r, hT[c].ap, A1p(c), MUL, [hT[c], dp], [hT[c]])
            ts(hT[c].r, hT[c].ap, B1p(c), ADD, [hT[c], modP], [hT[c]])
            import os as _os
            if c == 1 and _os.environ.get("SPIN"):
                for _ in range(int(_os.environ["SPIN"])):
                    ts(hT[0].r, hT[0].ap, 1.0, MUL, [hT[0]], [hT[0]])
            chk(f"hT{c}", hT[c].ap, hT[c])
        ppool.put(rs)

        smp = (b == 0 and with_samples)
        if smp:
            xs0 = ppool.get(); xs1 = ppool.get()
            for u_, hf in ((xs0, 0), (xs1, 1)):
                S.dve(lambda e, u_=u_: e.memset(u_.ap, 0.0), [], [u_])
                S.dma("pool", u_.ap[0:NS, :], x_s[:, hf * 512:(hf + 1) * 512], [], [u_])
            for kc in range(8):
                src = xs0 if kc < 4 else xs1
                t_ = pq.get()
                S.tr(t_.ap, src.ap[:, (kc % 4) * 128:(kc % 4 + 1) * 128], ident.ap, [src, ident], [t_])
                S.act(xsT.ap[:, kc, :], t_.ap[:, 0:NS], AF.Copy, reads=[t_], writes=[xsT])
                pq.put(t_)
            ppool.put(xs0, xs1)
            s_modnorm(hsT, "n1", 8, 0)
        chk("hT", hT[0].ap, hT[0])
        for c in range(4):
            pv_ = pfull.get(); pg_ = pfull.get()
            proj(c * 128, hT, pv_.ap, pv_, samp=(hsT, sA.ap[:, c, :], sA, "copy") if smp else None)
            proj(DC + c * 128, hT, pg_.ap, pg_, samp=(hsT, sA.ap[:, 4 + c, :], sA, "tanh") if smp else None)
            tg = ppool.get(); vh = ppool.get()
            S.act(tg.ap, pg_.ap, AF.Tanh, scale=0.5, reads=[pg_], writes=[tg])
            S.act(vh.ap, pv_.ap, AF.Identity, scale=0.5, reads=[pv_], writes=[vh])
            pfull.put(pv_, pg_)
            stt(gluext.r[:, c, 30:30 + TB], tg.ap, 1.0, vh.ap, ADD, MUL, [tg, vh], [gluext])
            ppool.put(tg, vh)
        if smp:
            exu = [ppool.get() for _ in range(4)]
            exv = [u_.ap[:, 0:31 * NS].rearrange("p (j s) -> p j s", s=NS) for u_ in exu]
            for g_ in range(4):
                stc = ppool.get()
                S.dve(lambda e, stc=stc: e.memset(stc.ap, 0.0), [], [stc])
                S.dma("pool", stc.ap[0:120, :], st_conf[4 * g_:4 * g_ + 4].rearrange("s j c -> (s j) c"), [], [stc])
                for c in range(4):
                    t_ = pq.get()
                    S.tr(t_.ap, stc.ap[:, c * 128:(c + 1) * 128], ident.ap, [stc, ident], [t_])
                    S.act(exv[c][:, 0:30, 4 * g_:4 * g_ + 4], t_.ap[:, 0:120].rearrange("p (s j) -> p j s", j=30),
                          AF.Copy, reads=[t_], writes=[exu[c]])
                    pq.put(t_)
                ppool.put(stc)
            for c in range(4):
                ts(sA.ap[:, c, :], sA.ap[:, c, :], 0.5, MUL, [sA], [sA])
                stt(exv[c][:, 30, :], sA.ap[:, 4 + c, :], 1.0, sA.ap[:, c, :], ADD, MUL, [sA], [exu[c]])
            S.dma("pool", nconf_s[:, 0:29, :], st_conf[:, 1:30, :], [], [], out_final=True)
            s_to_tm(lambda c: exv[c][:, 30, :], 4, lambda g0, ng: nconf_s[:, 29, g0 * 128:(g0 + ng) * 128], exu)
            acc = tppool.get()
            for c in range(4):
                ts(acc.ap[:, 0:NS], exv[c][:, 0, :], dwT.ap[:, 0, c:c + 1], MUL, [exu[c], dwT], [acc])
                for j in range(1, 31):
                    stt(acc.ap[:, 0:NS], exv[c][:, j, :], dwT.ap[:, j, c:c + 1], acc.ap[:, 0:NS], MUL, ADD,
                        [exu[c], dwT, acc], [acc])
                ts(a1s.ap[:, c, :], acc.ap[:, 0:NS], vcol("dwb", c), ADD, [acc, vecT], [a1s])
            tppool.put(acc); ppool.put(*exu)
            t_ = pq.get()
            for c in range(4):
                S.mm(t_.ap[:, 0:NS], ones.ap, a1s.ap[:, c, :], c == 0, c == 3, [ones, a1s], [t_])
            mean_ = tppool.get()
            S.act(mean_.ap[:, 0:NS], t_.ap[:, 0:NS], AF.Identity, scale=1.0 / DC, reads=[t_], writes=[mean_])
            pq.put(t_)
            for c in range(4):
                tt(a1s.ap[:, c, :], a1s.ap[:, c, :], mean_.ap[:, 0:NS], SUB, [a1s, mean_], [a1s])
            rs_ = s_rstd(a1s, 4, float(DC))
            for c in range(4):
                tt(a1s.ap[:, c, :], a1s.ap[:, c, :], rs_.ap[:, 0:NS], MUL, [a1s, rs_], [a1s])
                ts(a1s.ap[:, c, :], a1s.ap[:, c, :], vecT.ap[:, 85 + c:86 + c], MUL, [a1s, vecT], [a1s])
                ts(a1s.ap[:, c, :], a1s.ap[:, c, :], vecT.ap[:, 89 + c:90 + c], ADD, [a1s, vecT], [a1s])
            S.act(mean_.ap[:, 0:4 * NS], flat(a1s), AF.Tanh, reads=[a1s], writes=[mean_])
            stt(flat(aTs).bitcast(F32R), mean_.ap[:, 0:4 * NS], 1.0, flat(a1s), ADD, MUL, [mean_, a1s], [aTs])
            tppool.put(mean_, rs_)
        if last:
            pb = pfull.get()
            for c in range(4):
                S.tr(pb.ap[0:30, c * 128:(c + 1) * 128], gluext.ap[:, c, TB:TB + 30], ident.ap, [gluext, ident], [pb])
            o_ = ppool.get()
            S.act(o_.ap[0:30, :], pb.ap[0:30, :], AF.Copy, reads=[pb], writes=[o_])
            pfull.put(pb)
            S.dma("pool", nconf_p, o_.ap[0:30, :], [o_], [], out_final=True)
            ppool.put(o_)
        a1 = [rpool.get() for _ in range(4)]
        for c in range(4):
            dgu = [rpool.get() for _ in range(8)]
            for j in range(31):
                S.pool(lambda e, c=c, j=j, dgu=dgu: e.tensor_scalar(
                    out=dgu[j // 4].r[:, (j % 4) * 128:(j % 4 + 1) * 128], in0=ident.ap, scalar1=dwT.ap[:, j, c:c + 1],
                    scalar2=1.0, op0=MUL, op1=MUL), [ident, dwT], [dgu[j // 4]])
            pc = pfull.get()
            for j in range(31):
                S.mm(pc.ap, dgu[j // 4].r[:, (j % 4) * 128:(j % 4 + 1) * 128], gluext.r[:, c, j:j + TB], j == 0, j == 30,
                     [dgu[j // 4], gluext], [pc])
            rpool.put(*dgu)
            S.act(a1[c].r, pc.ap, AF.Identity, bias=vcol("dwb", c), reads=[pc, vecT], writes=[a1[c]])
            pfull.put(pc)
        for c in range(4):
            S.pool(lambda e, c=c: e.tensor_copy(out=gluext.r[:, c, 0:30], in_=gluext.ap[:, c, TB:TB + 30]),
                   [gluext], [gluext])
        pm = pfull.get()
        for c in range(4):
            S.mm(pm.ap, ones.r, a1[c].r, c == 0, c == 3, [ones, a1[c]], [pm])
        mean = ppool.get()
        S.act(mean.ap, pm.ap, AF.Identity, scale=1.0 / DC, reads=[pm], writes=[mean])
        pfull.put(pm)
        p2 = pfull.get()
        for c in range(4):
            sq = rpool.get()
            S.act(sq.r, a1[c].ap, AF.Square, reads=[a1[c]], writes=[sq])
            S.mm(p2.ap, ones.r, sq.r, c == 0, c == 3, [ones, sq], [p2])
            rpool.put(sq)
        msq = ppool.get(); var = ppool.get()
        tt(msq.ap, mean.ap, mean.ap, MUL, [mean], [msq])
        stt(var.ap, p2.ap, 1.0 / DC, msq.ap, MUL, SUB, [p2, msq], [var])
        pfull.put(p2); ppool.put(msq)
        S.act(var.ap, var.ap, AF.Ln, bias=epsb.ap, reads=[var, epsb], writes=[var])
        S.act(var.ap, var.ap, AF.Exp, scale=-0.5, reads=[var], writes=[var])
        aT = [rpool.get() for _ in range(4)]
        for c in range(4):
            d_ = ppool.get(); xh = ppool.get(); th = ppool.get()
            tt(d_.ap, a1[c].ap, mean.ap, SUB, [a1[c], mean], [d_])
            tt(d_.ap, d_.ap, var.ap, MUL, [d_, var], [d_])
            ts(xh.ap, d_.ap, vecT.ap[:, 85 + c:86 + c], MUL, [d_, vecT], [xh])
            ts(xh.ap, xh.ap, vecT.ap[:, 89 + c:90 + c], ADD, [xh, vecT], [xh])
            S.act(th.ap, xh.ap, AF.Tanh, reads=[xh], writes=[th])
            stt(aT[c].r, th.ap, 1.0, xh.ap, ADD, MUL, [th, xh], [aT[c]])
            ppool.put(d_, xh, th)
        rpool.put(*a1)
        ppool.put(mean, var)
        mg = [rpool.get() for _ in range(8)]
        for e_ in range(8):
            pga = pfull.get()
            proj(OFF_GA + e_ * 128, hT, pga.ap, pga, samp=(hsT, gS.ap[:, e_, :], gS, "tanh") if smp else None)
            tga = ppool.get()
            S.act(tga.ap, pga.ap, AF.Tanh, scale=0.5, reads=[pga], writes=[tga])
            pfull.put(pga)
            pya = pfull.get()
            proj(e_ * 128, aT, pya.ap, pya, nk=4, wview=w_co_v,
                 samp=(aTs, rawS.ap[:, e_, :], rawS, "copy") if smp else None)
            if smp:
                stt(mgS.r[:, e_, :], gS.ap[:, e_, :], 1.0, rawS.ap[:, e_, :], ADD, MUL, [gS, rawS], [mgS])
            stt(mg[e_].r, tga.ap, 1.0, pya.ap, ADD, MUL, [tga, pya], [mg[e_]])
            pfull.put(pya); ppool.put(tga)
        rpool.put(*aT)
        chk("mgA", mg[0].ap, mg[0])

        wba = load_w(w_in_v[:, :, OFF_B:OFF_B + 16], nk=8, ncol=16)
        pa_ = pfull.get()
        for kc in range(8):
            S.mm(pa_.ap[0:8, :], wba.r[:, kc, 8:16], hT[kc].r, kc == 0, kc == 7, [wba, hT[kc]], [pa_])
        xg = ppool.get(); ax = ppool.get(); Grow = ppool.get(); rmask = ppool.get()
        S.pool(lambda e, rmask=rmask: e.memset(rmask.ap[0:8, :], 1.0), [], [rmask])
        for i_ in range(8):
            S.pool(lambda e, rmask=rmask, i_=i_: e.memset(rmask.ap[0:8, i_ * 64:i_ * 64 + 1], 0.0), [rmask], [rmask])
        S.act(xg.ap[0:8, :], pa_.ap[0:8, :], AF.Identity, bias=hp8.ap[:, 1:2], reads=[pa_, hp8], writes=[xg])
        pfull.put(pa_)
        stt(ax.ap[0:8, :], xg.ap[0:8, :], -1.0, xg.ap[0:8, :], MUL, MAX, [xg], [ax])
        S.act(ax.ap[0:8, :], ax.ap[0:8, :], AF.Exp, scale=-1.0, reads=[ax], writes=[ax])
        S.act(ax.ap[0:8, :], ax.ap[0:8, :], AF.Ln, bias=1.0, reads=[ax], writes=[ax])
        stt(xg.ap[0:8, :], xg.ap[0:8, :], 0.0, ax.ap[0:8, :], MAX, ADD, [xg, ax], [xg])
        ts(xg.ap[0:8, :], xg.ap[0:8, :], hp8.ap[:, 2:3], MUL, [xg, hp8], [xg])
        S.dve(lambda e, xg=xg, Grow=Grow, rmask=rmask: e.tensor_tensor_scan(out=Grow.ap[0:8, :], data0=rmask.ap[0:8, :], data1=xg.ap[0:8, :],
                                                               initial=0.0, op0=MUL, op1=ADD), [rmask, xg], [Grow])
        ppool.put(xg, ax, rmask)
        if smp:
            t_ = pq.get()
            for kc in range(8):
                S.mm(t_.ap[0:8, 0:NS], wba.r[:, kc, 0:8], hsT.r[:, kc, :], kc == 0, kc == 7, [wba, hsT], [t_])
            S.act(ba8.ap[:, 0, :], t_.ap[0:8, 0:NS], AF.Tanh, scale=0.5, reads=[t_], writes=[ba8])
            pq.put(t_)
            t_ = pq.get()
            for kc in range(8):
                S.mm(t_.ap[0:8, 0:NS], wba.r[:, kc, 8:16], hsT.r[:, kc, :], kc == 0, kc == 7, [wba, hsT], [t_])
            S.act(ba8.ap[:, 1, :], t_.ap[0:8, 0:NS], AF.Identity, bias=hp8.ap[:, 1:2], reads=[t_, hp8], writes=[ba8])
            pq.put(t_)
            ts(ba8.ap[:, 0, :], ba8.ap[:, 0, :], 0.5, MUL, [ba8], [ba8], s2=0.5, op1=ADD)
            sp_ = tppool.get()
            stt(sp_.ap[0:8, 0:NS], ba8.ap[:, 1, :], -1.0, ba8.ap[:, 1, :], MUL, MAX, [ba8], [sp_])
            S.act(sp_.ap[0:8, 0:NS], sp_.ap[0:8, 0:NS], AF.Exp, scale=-1.0, reads=[sp_], writes=[sp_])
            S.act(sp_.ap[0:8, 0:NS], sp_.ap[0:8, 0:NS], AF.Ln, bias=1.0, reads=[sp_], writes=[sp_])
            stt(ba8.ap[:, 1, :], ba8.ap[:, 1, :], 0.0, sp_.ap[0:8, 0:NS], MAX, ADD, [ba8, sp_], [ba8])
            ts(ba8.ap[:, 1, :], ba8.ap[:, 1, :], hp8.ap[:, 2:3], MUL, [ba8, hp8], [ba8])
            tppool.put(sp_)
        for u in range(4):
            us = slice(u * 128, (u + 1) * 128)
            t_ = pq.get()
            for kc in range(8):
                S.mm(t_.ap[:, 0:16], hT[kc].r[:, us], wba.r[:, kc, 0:16], kc == 0, kc == 7, [wba, hT[kc]], [t_])
            ct = colT[u]
            S.act(ct.ap[:, 0:8], t_.ap[:, 0:8], AF.Tanh, scale=0.5, reads=[t_], writes=[ct])
            ts(ct.ap[:, 0:8], ct.ap[:, 0:8], 0.5, MUL, [ct], [ct], s2=0.5, op1=ADD)
            tt(ct.ap[:, 40:48], t_.ap[:, 8:16], hpb.ap[:, 1, :], ADD, [t_, hpb], [ct])
            pq.put(t_)
            stt(ct.ap[:, 32:40], ct.ap[:, 40:48], -1.0, ct.ap[:, 40:48], MUL, MAX, [ct], [ct])
            S.act(ct.ap[:, 32:40], ct.ap[:, 32:40], AF.Exp, scale=-1.0, reads=[ct], writes=[ct])
            S.act(ct.ap[:, 32:40], ct.ap[:, 32:40], AF.Ln, bias=1.0, reads=[ct], writes=[ct])
            stt(ct.ap[:, 40:48], ct.ap[:, 40:48], 0.0, ct.ap[:, 32:40], MAX, ADD, [ct], [ct])
            tt(ct.ap[:, 40:48], ct.ap[:, 40:48], hpb.ap[:, 2, :], MUL, [ct, hpb], [ct])
            t2 = pq.get()
            S.mm(t2.ap[:, 0:8], Mcum.ap, ct.ap[:, 40:48], True, True, [Mcum, ct], [t2])
            S.mm(t2.ap[:, 8:16], Maft.ap, ct.ap[:, 40:48], True, True, [Maft, ct], [t2])
            S.act(ct.ap[:, 8:16], t2.ap[:, 0:8], AF.Copy, reads=[t2], writes=[ct])
            S.act(ct.ap[:, 16:24], t2.ap[:, 0:8], AF.Identity, scale=-1.0, reads=[t2], writes=[ct])
            S.act(ct.ap[:, 24:32], t2.ap[:, 0:8], AF.Exp, reads=[t2], writes=[ct])
            S.act(ct.ap[:, 32:40], t2.ap[:, 8:16], AF.Exp, reads=[t2], writes=[ct])
            pq.put(t2)
            tt(ct.ap[:, 24:32], ct.ap[:, 24:32], ct.ap[:, 0:8], MUL, [ct], [ct])
        wpool.put(wba)
        chk("colT", colT[0].ap, colT[0], 128, 48)

        oT = [rpool.get() for _ in range(8)]
        def front_gen(h, out):
            ext = qkvext[0]
            for i in range(3):
                ch = i * 8 + h
                S.pool(lambda e, ext=ext, i=i, ch=ch: e.tensor_copy(out=ext.r[:, i, 0:3], in_=qkvhist.ap[:, ch, :]),
                       [qkvhist], [ext])
                pb = pfull.get()
                proj(OFF_Q + ch * 128, hT, pb.ap, pb, samp=(hsT, qkvS.ap[:, ch, :], qkvS, "copy") if smp else None)
                S.act(ext.r[:, i, 3:3 + TB], pb.ap, AF.Copy, reads=[pb], writes=[ext])
                pfull.put(pb)
                S.pool(lambda e, ext=ext, i=i, ch=ch: e.tensor_copy(out=qkvhist.r[:, ch, :], in_=ext.ap[:, i, TB:TB + 3]),
                       [ext], [qkvhist])
                yield
            if last:
                t_ = pq.get(); o_ = tppool.get()
                for i in range(3):
                    S.tr(t_.ap[0:3, :], ext.ap[:, i, TB:TB + 3], ident.ap, [ext, ident], [t_])
                    S.act(o_.ap[0:3, :], t_.ap[0:3, :], AF.Copy, reads=[t_], writes=[o_])
                    S.dma("pool", nqkv_p[:, (i * 8 + h) * 128:(i * 8 + h + 1) * 128], o_.ap[0:3, :], [o_], [],
                          out_final=True)
                pq.put(t_); tppool.put(o_)
            dgu = [rpool.get() for _ in range(3)]
            sil = []
            for i in range(3):
                ch = i * 8 + h
                for j in range(4):
                    S.pool(lambda e, i=i, j=j, ch=ch, dgu=dgu: e.tensor_scalar(
                        out=dgu[i].r[:, j * 128:(j + 1) * 128], in0=ident.ap, scalar1=gcT.ap[:, j, ch:ch + 1],
                        scalar2=1.0, op0=MUL, op1=MUL), [ident, gcT], [dgu[i]])
                pc = pfull.get()
                for j in range(4):
                    S.mm(pc.ap, dgu[i].r[:, j * 128:(j + 1) * 128], ext.r[:, i, j:j + TB], j == 0, j == 3, [dgu[i], ext], [pc])
                o_ = ppool.get()
                silu_from_psum(pc, o_.ap, o_)
                pfull.put(pc)
                sil.append(o_)
                yield
            rpool.put(*dgu)
            qs, ks, vs = sil
            pz = pfull.get()
            proj(OFF_Z + h * 128, hT, pz.ap, pz, samp=(hsT, zS.ap[:, h, :], zS, "copy") if smp else None)
            zs = ppool.get()
            silu_from_psum(pz, zs.ap, zs)
            pfull.put(pz)
            out.extend([qs, ks, vs, zs])

        def run_gens(gens):
            gens = list(gens)
            while gens:
                for g_ in list(gens):
                    try:
                        next(g_)
                    except StopIteration:
                        gens.remove(g_)

        def mid_gen(h, sil, out):
            qs, ks, vs, zs = sil
            pg = pfull.get()
            gsel = ppool.get()
            S.dve(lambda e, gsel=gsel: e.memset(gsel.ap, 0.0), [], [gsel])
            ts(gsel.ap[0:8, :], Grow.ap[0:8, :], ident.ap[0:8, h:h + 1], MUL, [Grow, ident], [gsel])
            S.mm(pg.ap, ones.ap, gsel.ap, True, True, [ones, gsel], [pg])
            ppool.put(gsel)
            gam = ppool.get(); argL = ppool.get(); argU = ppool.get()
            S.act(gam.ap, pg.ap, AF.Exp, reads=[pg], writes=[gam])
            tt(argL.ap, pg.ap, mL4.ap, ADD, [pg, mL4], [argL])
            tt(argU.ap, pg.ap, mU4.ap, SUB, [pg, mU4], [argU])
            pfull.put(pg)
            yield
            Dst = [tppool.get() for _ in range(4)]; DT = [tppool.get() for _ in range(4)]
            for u in range(4):
                us = slice(u * 128, (u + 1) * 128)
                S.act(Dst[u].ap, argL.ap[:, us], AF.Exp, scale=-1.0, bias=colT[u].ap[:, 8 + h:9 + h],
                      reads=[argL, colT[u]], writes=[Dst[u]])
                S.act(DT[u].ap, argU.ap[:, us], AF.Exp, scale=1.0, bias=colT[u].ap[:, 16 + h:17 + h],
                      reads=[argU, colT[u]], writes=[DT[u]])
                yield
            ppool.put(argL, argU)
            rq = rstd_from_chunks([(qs.ap, qs)], 1.0)
            stt(qs.ap, qs.ap, 128.0 ** -0.5, rq.ap, MUL, MUL, [qs, rq], [qs])
            ppool.put(rq)
            yield
            rk = rstd_from_chunks([(ks.ap, ks)], 1.0)
            tt(ks.ap, ks.ap, rk.ap, MUL, [ks, rk], [ks], eng="pool")
            ppool.put(rk)
            yield
            gq = ppool.get()
            tt(gq.ap, qs.ap, gam.ap, MUL, [qs, gam], [gq], eng="pool")
            out.extend([gam, Dst, DT, gq])

        def nxt_gen(h, sil, out):
            yield from front_gen(h, sil)
            yield from mid_gen(h, sil, out)

        sil_next = []; mid_next = []
        run_gens([nxt_gen(0, sil_next, mid_next)])
        for h in range(NH):
            ext = qkvext[0]
            qs, ks, vs, zs = sil_next
            gam, Dst, DT, gq = mid_next
            sil_next = []; mid_next = []
            valkc = [None] * 4; kcdT = [None] * 4; attT = [None] * 4

            def unit_gen(u):
                us = slice(u * 128, (u + 1) * 128)
                ct = colT[u]
                pA = pq.get()
                S.mm(pA.ap, ks.ap[:, us], ks.ap[:, us], True, True, [ks], [pA])
                Lt = trpool.get()
                stt(Lt.ap, pA.ap, ct.ap[:, h:h + 1], Dst[u].ap, MUL, MUL, [pA, ct, Dst[u]], [Lt])
                pq.put(pA)
                yield
                pT = pq.get()
                S.mm(pT.ap, ks.ap[:, us], qs.ap[:, us], True, True, [ks, qs], [pT])
                at = tppool.get()
                tt(at.ap, pT.ap, DT[u].ap, MUL, [pT, DT[u]], [at])
                pq.put(pT)
                attT[u] = at
                yield
                cur = r2pool.get()
                pk = pq.get()
                S.tr(pk.ap, ks.ap[:, us], ident.ap, [ks, ident], [pk])
                ts(cur.ap[:, 128:256], pk.ap, ct.ap[:, 24 + h:25 + h], MUL, [pk, ct], [cur])
                ts(KdA[u].ap[0:64, :], pk.ap[0:64, :], ct.ap[0:64, 32 + h:33 + h], MUL, [pk, ct], [KdA[u]])
                ts(KdB[u].ap[64:128, :], pk.ap[64:128, :], ct.ap[64:128, 32 + h:33 + h], MUL, [pk, ct], [KdB[u]])
                pq.put(pk)
                yield
                pv = pq.get()
                S.tr(pv.ap, vs.ap[:, us], ident.ap, [vs, ident], [pv])
                ts(cur.ap[:, 0:128], pv.ap, ct.ap[:, h:h + 1], MUL, [pv, ct], [cur])
                pq.put(pv)
                yield
                pl = pq.get()
                S.tr(pl.ap, Lt.ap, ident.ap, [Lt, ident], [pl])
                LTf = trpool.get()
                S.act(LTf.ap, pl.ap, AF.Copy, reads=[pl], writes=[LTf])
                pq.put(pl)
                yield
                Ld = trpool.get(); LdT = trpool.get(); BT = trpool.get()
                tt(Ld.ap, Lt.ap, M32.ap, MUL, [Lt, M32], [Ld], eng="pool")
                tt(LdT.ap, LTf.ap, M32.ap, MUL, [LTf, M32], [LdT], eng="pool")
                tt(BT.ap, LTf.ap, LdT.ap, SUB, [LTf, LdT], [BT], eng="pool")
                trpool.put(Lt, LTf)
                yield
                Mt = trpool.get()
                tt(Mt.ap, ident.ap, LdT.ap, SUB, [ident, LdT], [Mt], eng="pool")
                P, PT = Ld, LdT
                for lev in range(4):
                    p2_ = pq.get()
                    S.mm(p2_.ap, PT.ap, P.ap, True, True, [P, PT], [p2_])
                    P2 = trpool.get()
                    S.dve(lambda e, P2=P2, p2_=p2_: e.tensor_copy(out=P2.ap, in_=p2_.ap), [p2_], [P2])
                    pq.put(p2_)
                    PT2 = None
                    if lev < 3:
                        pT2 = pq.get()
                        S.mm(pT2.ap, P.ap, PT.ap, True, True, [P, PT], [pT2])
                        PT2 = trpool.get()
                        S.act(PT2.ap, pT2.ap, AF.Copy, reads=[pT2], writes=[PT2])
                        pq.put(pT2)
                    trpool.put(P, PT)
                    yield
                    pm_ = pq.get()
                    S.mm(pm_.ap, P2.ap, Mt.ap, True, True, [P2, Mt], [pm_])
                    tt(Mt.ap, Mt.ap, pm_.ap, ADD, [Mt, pm_], [Mt])
                    pq.put(pm_)
                    P, PT = P2, PT2
                    yield
                trpool.put(P)
                ph = phalf.get()
                S.mm(ph.ap, Mt.ap, cur.ap, True, True, [Mt, cur], [ph])
                Y_ = r2pool.get()
                S.act(Y_.ap, ph.ap, AF.Copy, reads=[ph], writes=[Y_])
                phalf.put(ph); r2pool.put(cur)
                yield
                ph = phalf.get()
                S.mm(ph.ap, BT.ap, Y_.ap, True, True, [BT, Y_], [ph])
                Z_ = r2pool.get()
                S.act(Z_.ap, ph.ap, AF.Copy, reads=[ph], writes=[Z_])
                phalf.put(ph)
                yield
                ph = phalf.get()
                S.mm(ph.ap, Mt.ap, Z_.ap, True, True, [Mt, Z_], [ph])
                cur = r2pool.get()
                tt(cur.ap, Y_.ap, ph.ap, SUB, [Y_, ph], [cur])
                phalf.put(ph); r2pool.put(Y_, Z_)
                trpool.put(BT, Mt)
                yield
                pk2 = pq.get()
                S.tr(pk2.ap, cur.ap[:, 128:256], ident.ap, [cur, ident], [pk2])
                kt = tppool.get()
                S.act(kt.ap, pk2.ap, AF.Copy, reads=[pk2], writes=[kt])
                pq.put(pk2)
                kcdT[u] = kt; valkc[u] = cur

            for grp in ((0, 1, 2, 3),):
                gens = [unit_gen(u) for u in grp]
                while gens:
                    for g_ in list(gens):
                        try:
                            next(g_)
                        except StopIteration:
                            gens.remove(g_)
            tppool.put(*Dst); tppool.put(*DT)
            ppool.put(qs, ks, vs)
            po = pfull.get()

            def scan_gen():
                for u in range(4):
                    for hf in range(2):
                        r0_ = hf * 64
                        cs = slice(u * 128 + r0_, u * 128 + r0_ + 64)
                        pP = pq.get()
                        S.mm(pP.ap, kcdT[u].ap, Sst[h].ap, True, True, [kcdT[u], Sst[h]], [pP])
                        tt(vnew.ap[r0_:r0_ + 64, :], valkc[u].ap[r0_:r0_ + 64, 0:128], pP.ap[r0_:r0_ + 64, :], SUB,
                           [valkc[u], pP], [vnew])
                        pq.put(pP)
                        S.mm(po.ap[:, cs], Sst[h].ap, gq.ap[:, cs], True, False, [Sst[h], gq], [po])
                        S.mm(po.ap[:, cs], vnew.ap, attT[u].ap[:, r0_:r0_ + 64], False, True, [vnew, attT[u]], [po])
                        pS = pq.get()
                        Kd = KdA[u] if hf == 0 else KdB[u]
                        S.mm(pS.ap, Kd.ap, vnew.ap, True, True, [Kd, vnew], [pS])
                        gl = u * 128 + r0_ + 63
                        stt(Sst[h].ap, Sst[h].ap, gam.ap[:, gl:gl + 1], pS.ap, MUL, ADD, [Sst[h], gam, pS], [Sst[h]])
                        pq.put(pS)
                        yield
            sg_ = scan_gen()
            gl_ = [sg_, sg_]
            if h + 1 < NH:
                gl_.append(nxt_gen(h + 1, sil_next, mid_next))
            if b == 0:
                gl_.append(modB)
            run_gens(gl_)
            for u in range(4):
                r2pool.put(valkc[u]); tppool.put(kcdT[u], attT[u])
            ppool.put(gq, gam)
            osb = ppool.get()
            S.act(osb.ap, po.ap, AF.Copy, reads=[po], writes=[osb])
            pfull.put(po)
            ro = rstd_from_chunks([(osb.ap, osb)], 128.0)
            stt(osb.ap, osb.ap, vcol("gnw"), ro.ap, MUL, MUL, [osb, vecT, ro], [osb])
            tt(oT[h].r, osb.ap, zs.ap, MUL, [osb, zs], [oT[h]])
            ppool.put(osb, ro, zs)
            if last:
                S.dma("pool", ndelta_p[h], Sst[h].ap, [Sst[h]], [], out_final=True)
        ppool.put(Grow)
        if b == 0:
            finish_modB()
        if smp:
            exq = [ppool.get() for _ in range(3)]
            exqv = [u_.ap.rearrange("p (c j s) -> p c j s", j=4, s=NS) for u_ in exq]
            for g_ in range(6):
                stq = ppool.get()
                S.dve(lambda e, stq=stq: e.memset(stq.ap, 0.0), [], [stq])
                S.dma("pool", stq.ap[0:48, :], st_qkv[:, :, g_ * 512:(g_ + 1) * 512].rearrange("s j c -> (s j) c"), [], [stq])
                for cc in range(4):
                    ch = g_ * 4 + cc
                    t_ = pq.get()
                    S.tr(t_.ap, stq.ap[:, cc * 128:(cc + 1) * 128], ident.ap, [stq, ident], [t_])
                    S.act(exqv[ch // 8][:, ch % 8, 0:3, :], t_.ap[:, 0:48].rearrange("p (s j) -> p j s", j=3),
                          AF.Copy, reads=[t_], writes=[exq[ch // 8]])
                    pq.put(t_)
                ppool.put(stq)
            S.dma("pool", nqkv_s[:, 0:2, :], st_qkv[:, 1:3, :], [], [], out_final=True)
            s_to_tm(lambda c: qkvS.ap[:, c, :], 24, lambda g0, ng: nqkv_s[:, 2, g0 * 128:(g0 + ng) * 128], [qkvS])
            for ch in range(24):
                ev = exqv[ch // 8]
                S.act(ev[:, ch % 8, 3, :], qkvS.ap[:, ch, :], AF.Copy, reads=[qkvS], writes=[exq[ch // 8]])
                ts(qkvC.ap[:, ch, :], ev[:, ch % 8, 0, :], gcT.ap[:, 0, ch:ch + 1], MUL, [exq[ch // 8], gcT], [qkvC])
                for j in range(1, 4):
                    stt(qkvC.ap[:, ch, :], ev[:, ch % 8, j, :], gcT.ap[:, j, ch:ch + 1], qkvC.ap[:, ch, :], MUL, ADD,
                        [exq[ch // 8], gcT, qkvC], [qkvC])
            ppool.put(*exq)
            th_ = ppool.get()
            ts(flat(qkvC), flat(qkvC), 0.5, MUL, [qkvC], [qkvC])
            S.act(th_.ap[:, 0:24 * NS], flat(qkvC), AF.Tanh, reads=[qkvC], writes=[th_])
            stt(flat(qkvC), th_.ap[:, 0:24 * NS], 1.0, flat(qkvC), ADD, MUL, [th_, qkvC], [qkvC])
            ppool.put(th_)
            qf, kf, vf = flat(qkvC, 0, 8), flat(qkvC, 8, 16), flat(qkvC, 16, 24)

            def bc_sum(src_ap, reads_):
                t_ = pq.get()
                S.mm(t_.ap, ones.ap, src_ap, True, True, [ones] + reads_, [t_])
                o_ = tppool.get()
                S.act(o_.ap, t_.ap, AF.Copy, reads=[t_], writes=[o_])
                pq.put(t_)
                return o_
            tmp_ = tppool.get()
            for f_, scl in ((qf, 128.0 ** -0.5), (kf, 1.0)):
                tt(tmp_.ap, f_, f_, MUL, [qkvC], [tmp_])
                ss_ = bc_sum(tmp_.ap, [tmp_])
                S.act(ss_.ap, ss_.ap, AF.Ln, bias=epsb.ap, reads=[ss_, epsb], writes=[ss_])
                S.act(ss_.ap, ss_.ap, AF.Exp, scale=-0.5, reads=[ss_], writes=[ss_])
                stt(f_, f_, scl, ss_.ap, MUL, MUL, [qkvC, ss_], [qkvC])
                tppool.put(ss_)
            E_ = tppool.get()
            bcs = []
            for row in range(2):
                S.dve(lambda e, E_=E_: e.memset(E_.ap, 0.0), [], [E_])
                for h in range(NH):
                    ts(E_.ap[0:8, h * NS:(h + 1) * NS], ba8.ap[:, row, :], ident.ap[0:8, h:h + 1], MUL, [ba8, ident], [E_])
                bcs.append(bc_sum(E_.ap, [E_]))
            tppool.put(E_)
            bbc, gbc = bcs
            S.act(gbc.ap, gbc.ap, AF.Exp, reads=[gbc], writes=[gbc])
            Wq = tppool.get(); Wk = tppool.get(); bv = tppool.get(); vnT = tppool.get()
            tt(Wq.ap, qf, gbc.ap, MUL, [qkvC, gbc], [Wq])
            tt(Wk.ap, kf, gbc.ap, MUL, [qkvC, gbc], [Wk])
            stt(Wk.ap, Wk.ap, -1.0, bbc.ap, MUL, MUL, [Wk, bbc], [Wk])
            tt(bv.ap, vf, bbc.ap, MUL, [qkvC, bbc], [bv])
            O1 = pq.get(); O2 = pq.get()
            grp_l = [(h, g_) for h in range(NH) for g_ in range(4)]
            loaded = {}

            def load_state(i_):
                h_, g2 = grp_l[i_]
                u_ = ppool.get()
                S.dma("sp", u_.ap.rearrange("p (s v) -> p s v", v=128),
                      st_delta[4 * g2:4 * g2 + 4, h_].rearrange("s k v -> k s v"), [], [u_])
                loaded[i_] = u_
            PF = 3
            for i_ in range(PF):
                load_state(i_)
            for gi, (h, g_) in enumerate(grp_l):
                if True:
                    if gi + PF < len(grp_l):
                        load_state(gi + PF)
                    S4 = loaded.pop(gi)
                    S4v = S4.ap.rearrange("p (s v) -> p s v", v=128)
                    for j in range(4):
                        p_ = h * NS + 4 * g_ + j
                        S.mm(O1.ap[:, p_:p_ + 1], S4v[:, j, :], Wq.ap[:, p_:p_ + 1], True, True, [S4, Wq], [O1])
                        S.mm(O2.ap[:, p_:p_ + 1], S4v[:, j, :], Wk.ap[:, p_:p_ + 1], True, True, [S4, Wk], [O2])
                    c0_ = h * NS + 4 * g_
                    tt(vnT.ap[:, c0_:c0_ + 4], bv.ap[:, c0_:c0_ + 4], O2.ap[:, c0_:c0_ + 4], ADD, [bv, O2], [vnT])
                    Vps = [tppool.get() for _ in range(4)]
                    pvs = [pq.get() for _ in range(4)]
                    for j in range(4):
                        p_ = h * NS + 4 * g_ + j
                        ts(Vps[j].ap, ones.ap, vnT.ap[:, p_:p_ + 1], MUL, [ones, vnT], [Vps[j]])
                    for j in range(4):
                        S.mm(pvs[j].ap, Vps[j].ap, ident.ap, True, True, [Vps[j], ident], [pvs[j]])
                    for j in range(4):
                        p_ = h * NS + 4 * g_ + j
                        ts(Vps[j].ap, pvs[j].ap, kf[:, p_:p_ + 1], MUL, [pvs[j], qkvC], [Vps[j]])
                        stt(S4v[:, j, :], S4v[:, j, :], gbc.ap[:, p_:p_ + 1], Vps[j].ap, MUL, ADD, [S4, gbc, Vps[j]], [S4])
                    pq.put(*pvs); tppool.put(*Vps)
                    S.dma("pool", ndelta_s[4 * g_:4 * g_ + 4, h].rearrange("s k v -> k s v"), S4v, [S4], [], out_final=True)
                    ppool.put(S4)
            tt(tmp_.ap, qf, kf, MUL, [qkvC], [tmp_])
            qk_ = bc_sum(tmp_.ap, [tmp_])
            tt(qk_.ap, qk_.ap, vnT.ap, MUL, [qk_, vnT], [qk_])
            tt(qk_.ap, qk_.ap, O1.ap, ADD, [qk_, O1], [qk_])
            pq.put(O1, O2)
            tt(tmp_.ap, qk_.ap, qk_.ap, MUL, [qk_], [tmp_])
            ss_ = bc_sum(tmp_.ap, [tmp_])
            S.act(ss_.ap, ss_.ap, AF.Ln, scale=1.0 / 128.0, bias=epsb.ap, reads=[ss_, epsb], writes=[ss_])
            S.act(ss_.ap, ss_.ap, AF.Exp, scale=-0.5, reads=[ss_], writes=[ss_])
            stt(qk_.ap, qk_.ap, vcol("gnw"), ss_.ap, MUL, MUL, [qk_, vecT, ss_], [qk_])
            ts(flat(zS), flat(zS), 0.5, MUL, [zS], [zS])
            S.act(tmp_.ap, flat(zS), AF.Tanh, reads=[zS], writes=[tmp_])
            stt(tmp_.ap, tmp_.ap, 1.0, flat(zS), ADD, MUL, [tmp_, zS], [tmp_])
            tt(flat(oTs).bitcast(F32R), qk_.ap, tmp_.ap, MUL, [qk_, tmp_], [oTs])
            tppool.put(tmp_, ss_, qk_, bbc, gbc, Wq, Wk, bv, vnT)
        chk("oT", oT[0].ap, oT[0])

        for e_ in range(8):
            pgb = pfull.get()
            proj(OFF_GB + e_ * 128, hT, pgb.ap, pgb, samp=(hsT, gS.ap[:, e_, :], gS, "tanh") if smp else None)
            tgb = ppool.get()
            S.act(tgb.ap, pgb.ap, AF.Tanh, scale=0.5, reads=[pgb], writes=[tgb])
            pfull.put(pgb)
            pyb = pfull.get()
            proj(e_ * 128, oT, pyb.ap, pyb, wview=w_go_v, samp=(oTs, rawS.ap[:, e_, :], rawS, "copy") if smp else None)
            if smp:
                stt(rawS.ap[:, e_, :], gS.ap[:, e_, :], 1.0, rawS.ap[:, e_, :], ADD, MUL, [gS, rawS], [rawS])
                tt(mgS.r[:, e_, :], mgS.ap[:, e_, :], rawS.ap[:, e_, :], ADD, [mgS, rawS], [mgS])
            stt(tgb.ap, tgb.ap, 1.0, pyb.ap, ADD, MUL, [tgb, pyb], [tgb])
            pfull.put(pyb)
            tt(mg[e_].r, mg[e_].ap, tgb.ap, ADD, [mg[e_], tgb], [mg[e_]])
            ppool.put(tgb)
        rpool.put(*oT)
        for e_ in range(8):
            pat = pfull.get()
            proj(e_ * 128, mg, pat.ap, pat, wview=w_o_v, samp=(mgS, rawS.ap[:, e_, :], rawS, "copy") if smp else None)
            if smp:
                tt(rawS.ap[:, e_, :], rawS.ap[:, e_, :], modS.ap[:, 16 + e_, :], MUL, [rawS, modS], [rawS])
                stt(xsT.ap[:, e_, :], rawS.ap[:, e_, :], 0.5, xsT.ap[:, e_, :], MUL, ADD, [rawS, xsT], [xsT])
            stt(xT[e_].ap, pat.ap, HG1p(e_), xT[e_].ap, MUL, ADD, [pat, dp, xT[e_]], [xT[e_]])
            pfull.put(pat)
        rpool.put(*mg)
        chk("x1", xT[0].ap, xT[0])

        rs = rstd_from_chunks([(xT[c].ap, xT[c]) for c in range(8)], float(D))
        for c in range(8):
            tt(hT[c].r, xT[c].ap, rs.ap, MUL, [xT[c], rs], [hT[c]])
            ts(hT[c].r, hT[c].ap, A2p(c), MUL, [hT[c], dp], [hT[c]])
            ts(hT[c].r, hT[c].ap, B2p(c), ADD, [hT[c], modP], [hT[c]])
        ppool.put(rs)
        if smp:
            s_modnorm(hsT, "n2", 32, 24)
        for half in range(2):
            fT = [rpool.get() for _ in range(16)]
            for fc in range(16):
                pf = pfull.get()
                proj((half * 16 + fc) * 128, hT, pf.ap, pf, wview=w_f1_v,
                     samp=(hsT, sA.ap[:, fc % 8, :], sA, "relu") if smp else None)
                if smp:
                    tt(fS.r[:, half * 16 + fc, :], sA.ap[:, fc % 8, :], sA.ap[:, fc % 8, :], MUL, [sA], [fS])
                r_ = ppool.get()
                S.act(r_.ap, pf.ap, AF.Relu, reads=[pf], writes=[r_])
                pfull.put(pf)
                S.pool(lambda e, fT=fT, fc=fc, r_=r_: e.tensor_tensor(out=fT[fc].r, in0=r_.ap, in1=r_.ap, op=MUL),
                       [r_], [fT[fc]])
                ppool.put(r_)
            for e_ in range(8):
                pf2 = pfull.get()
                for g_ in range(2):
                    wt = load_w(w_f2_v[:, half * 16 + g_ * 8:half * 16 + g_ * 8 + 8, e_ * 128:(e_ + 1) * 128])
                    for kc in range(8):
                        S.mm(pf2.ap, wt.r[:, kc, :], fT[g_ * 8 + kc].r, g_ == 0 and kc == 0, g_ == 1 and kc == 7,
                             [wt, fT[g_ * 8 + kc]], [pf2])
                    if smp:
                        if g_ == 0:
                            ts_ = pq.get()
                        for kc in range(8):
                            S.mm(ts_.ap[:, 0:NS], wt.r[:, kc, :], fS.r[:, half * 16 + g_ * 8 + kc, :],
                                 g_ == 0 and kc == 0, g_ == 1 and kc == 7, [wt, fS], [ts_])
                        if g_ == 1:
                            tt(rawS.ap[:, e_, :], ts_.ap[:, 0:NS], modS.ap[:, 40 + e_, :], MUL, [ts_, modS], [rawS])
                            tt(xsT.ap[:, e_, :], xsT.ap[:, e_, :], rawS.ap[:, e_, :], ADD, [xsT, rawS], [xsT])
                            pq.put(ts_)
                    wpool.put(wt)
                stt(xT[e_].ap, pf2.ap, G2p(e_), xT[e_].ap, MUL, ADD, [pf2, modP, xT[e_]], [xT[e_]])
                pfull.put(pf2)
            rpool.put(*fT)
        rpool.put(*hT)
        chk("x2", xT[0].ap, xT[0])

        rs = rstd_from_chunks([(xT[c].ap, xT[c]) for c in range(8)], float(D))
        for c in range(8):
            stt(xT[c].ap, xT[c].ap, vcol("nf", c), rs.ap, MUL, MUL, [xT[c], vecT, rs], [xT[c]])
        ppool.put(rs)
        for s_ in range(4):
            for hf in range(2):
                pb = pfull.get()
                for cc in range(4):
                    c = hf * 4 + cc
                    S.tr(pb.ap[:, cc * 128:(cc + 1) * 128], xT[c].ap[:, s_ * 128:(s_ + 1) * 128], ident.ap,
                         [xT[c], ident], [pb])
                o_ = ppool.get()
                if hf == 0:
                    S.act(o_.ap, pb.ap, AF.Copy, reads=[pb], writes=[o_])
                else:
                    S.dve(lambda e, o_=o_, pb=pb: e.tensor_copy(out=o_.ap, in_=pb.ap), [pb], [o_])
                pfull.put(pb)
                S.dma("pool", y_p[t0 + s_ * 128:t0 + (s_ + 1) * 128, hf * 512:(hf + 1) * 512], o_.ap, [o_], [],
                      out_final=True)
                ppool.put(o_)
        ppool.put(*xT)
        if smp:
            rs_ = s_rstd(xsT, 8, float(D))
            for kc in range(8):
                ts(rawS.ap[:, kc, :], xsT.ap[:, kc, :], vcol("nf", kc), MUL, [xsT, vecT], [rawS])
                tt(rawS.ap[:, kc, :], rawS.ap[:, kc, :], rs_.ap[:, 0:NS], MUL, [rawS, rs_], [rawS])
            tppool.put(rs_)
            s_to_tm(lambda c: rawS.ap[:, c, :], 8, lambda g0, ng: y_s[:, g0 * 128:(g0 + ng) * 128], [rawS])


_CACHE = {}


def _prep_inputs(inputs):
    f = lambda a: np.ascontiguousarray(np.asarray(a, dtype=np.float32))
    g = {k: f(v) for k, v in inputs.items()}
    shared = {
        "w_ada": g["w_ada"][0], "b_ada": g["b_ada"][0], "norm1_w": g["norm1_w"][0], "w_in": g["w_in"][0],
        "conf_dw_w": g["conf_dw_w"][0], "conf_dw_b": g["conf_dw_b"][0], "conf_ln_w": g["conf_ln_w"][0],
        "conf_ln_b": g["conf_ln_b"][0], "w_conf_out": g["w_conf_out"][0], "gdn_conv_w": g["gdn_conv_w"][0],
        "a_log": g["a_log"][0], "dt_bias": g["dt_bias"][0], "gdn_norm_w": g["gdn_norm_w"][0],
        "w_gdn_out": g["w_gdn_out"][0], "w_o": g["w_o"][0], "norm2_w": g["norm2_w"][0], "w_ff1": g["w_ff1"][0],
        "w_ff2": g["w_ff2"][0], "final_norm_w": g["final_norm_w"],
    }
    for nm in ("w_ada", "w_in", "w_conf_out", "w_gdn_out", "w_o", "w_ff1", "w_ff2"):
        shared[nm] = np.ascontiguousarray(np.pad(shared[nm], ((0, 0), (0, 128))))
    in_maps = []
    for c in range(8):
        sl = slice(c * NS, (c + 1) * NS)
        m = dict(shared)
        m["x_p"] = g["x_prompt"][c]
        m["x_s"] = np.ascontiguousarray(g["x_sample"][sl, 0, :])
        m["c_all"] = np.ascontiguousarray(np.concatenate([g["c_prompt"][c:c + 1], g["c_sample"][sl]], axis=0))
        m["st_conf"] = np.ascontiguousarray(g["state_conf_conv"][0, sl])
        m["st_qkv"] = np.ascontiguousarray(g["state_qkv_conv"][0, sl])
        m["st_delta"] = np.ascontiguousarray(g["state_delta"][0, sl])
        in_maps.append(m)
    return in_maps


def kernel(**inputs):
    if "nc" not in _CACHE:
        _CACHE["nc"] = build_program()
    nc = _CACHE["nc"]
    in_maps = _prep_inputs(inputs)
    res = run_bass_kernel_spmd(nc, in_maps, core_ids=list(range(8)))
    r = res.results
    y_prompt = np.stack([r[c]["y_p"] for c in range(8)], axis=0)
    y_sample = np.concatenate([r[c]["y_s"] for c in range(8)], axis=0)[:, None, :]
    nconf_p = np.stack([r[c]["nconf_p"] for c in range(8)], axis=0)[None]
    nqkv_p = np.stack([r[c]["nqkv_p"] for c in range(8)], axis=0)[None]
    ndelta_p = np.stack([r[c]["ndelta_p"] for c in range(8)], axis=0)[None]
    nconf_s = np.concatenate([r[c]["nconf_s"] for c in range(8)], axis=0)[None]
    nqkv_s = np.concatenate([r[c]["nqkv_s"] for c in range(8)], axis=0)[None]
    ndelta_s = np.concatenate([r[c]["ndelta_s"] for c in range(8)], axis=0)[None]
    return tuple(np.ascontiguousarray(a, dtype=np.float32) for a in
                 (y_prompt, y_sample, nconf_p, nqkv_p, ndelta_p, nconf_s, nqkv_s, ndelta_s))
```

```python
import numpy as np
import concourse.bass as bass
import concourse.mybir as mybir
from concourse.bass_utils import run_bass_kernel_spmd

F32 = mybir.dt.float32
F32R = mybir.dt.float32r
AF = mybir.ActivationFunctionType
ALU = mybir.AluOpType

SAME_ENGINE_SYNC = True


class Buf:
    def __init__(self, name, ap, root=None, excl=False):
        self.name = name
        self.ap = ap
        self.last_w = None
        self.readers = []
        self.root = root if root is not None else self
        self.excl = excl

    def __getitem__(self, idx):
        return self.ap[idx]

    @property
    def r(self):
        return self.ap.bitcast(F32R)


class Op:
    __slots__ = ("eng", "fn", "waits", "needed", "count", "is_dma", "sem", "semval", "idx")

    def __init__(self, eng, fn):
        self.eng = eng
        self.fn = fn
        self.waits = []
        self.needed = False
        self.count = 0
        self.is_dma = False
        self.sem = None
        self.semval = 0


class Sched:
    ENGS = ("pe", "act", "dve", "pool", "sp")

    def __init__(self, nc, n_dma_sems=(("sp", 28), ("pool", 16), ("act", 8))):
        self.nc = nc
        self.ops = {e: [] for e in self.ENGS}
        self.ctx = []
        self.esem = {}
        for e in self.ENGS:
            self.esem[e] = nc.alloc_semaphore(name=f"es_{e}")
        self.dsem = {}
        for q, n in n_dma_sems:
            self.dsem[q] = [[nc.alloc_semaphore(name=f"ds_{q}{i}"), 0, None] for i in range(n)]
        self.dcnt = {q: 0 for q, _ in n_dma_sems}
        self.out_dmas = []
        self.nbuf = 0

    def sb(self, name, shape, dtype=F32):
        self.nbuf += 1
        t = self.nc.alloc_sbuf_tensor(f"{name}_{self.nbuf}", list(shape), dtype)
        return Buf(name, t[:])

    def ps(self, name, shape, dtype=F32):
        self.nbuf += 1
        t = self.nc.alloc_psum_tensor(f"{name}_{self.nbuf}", list(shape), dtype)
        return Buf(name, t[:], excl=True)

    def _add(self, eng, fn, reads, writes):
        op = Op(eng, fn)
        deps = []
        rr, ww = [], []
        for w in writes:
            if w.root not in ww:
                ww.append(w.root)
        for r in reads:
            r = r.root
            if r.excl:
                if r not in ww:
                    ww.append(r)
            elif r not in rr:
                rr.append(r)
        reads, writes = rr, ww
        for r in reads:
            if r.last_w is not None:
                deps.append(r.last_w)
        for w in writes:
            if w.last_w is not None:
                deps.append(w.last_w)
            deps.extend(w.readers)
        seen = set()
        latest = {}
        for d in deps:
            if id(d) in seen or d is op:
                continue
            seen.add(id(d))
            if d.is_dma:
                op.waits.append(d)
                continue
            if d.eng == eng and (eng == "pe" or not SAME_ENGINE_SYNC):
                continue
            cur = latest.get(d.eng)
            if cur is None or d.idx > cur.idx:
                latest[d.eng] = d
        for d in latest.values():
            d.needed = True
            op.waits.append(d)
        for w in writes:
            w.last_w = op
            w.readers = []
        for r in reads:
            if r not in writes:
                r.readers.append(op)
        op.idx = len(self.ops[eng])
        self.ops[eng].append(op)
        return op

    def mm(self, out, lhsT, rhs, start, stop, reads, writes):
        return self._add("pe", lambda e: e.matmul(out, lhsT, rhs, start=start, stop=stop), reads, writes)

    def tr(self, out, in_, ident, reads, writes):
        return self._add("pe", lambda e: e.transpose(out, in_, ident), reads, writes)

    def act(self, out, in_, func, scale=1.0, bias=0.0, reads=(), writes=()):
        return self._add("act", lambda e: e.activation(out=out, in_=in_, func=func, bias=bias, scale=scale),
                         reads, writes)

    def dve(self, fn, reads, writes):
        return self._add("dve", fn, reads, writes)

    def pool(self, fn, reads, writes):
        return self._add("pool", fn, reads, writes)

    def any(self, eng, fn, reads, writes):
        return self._add(eng, fn, reads, writes)

    def dma(self, q, out, in_, reads, writes, out_final=False, precook=True):
        nc = self.nc

        def f(e):
            nc.dge_precook = precook
            ins = e.dma_start(out=out, in_=in_)
            nc.dge_precook = True
            return ins
        op = self._add(q, f, reads, writes)
        op.is_dma = True
        slots = self.dsem[q]
        k = self.dcnt[q] % len(slots)
        self.dcnt[q] += 1
        slot = slots[k]
        if slot[2] is not None:
            op.waits.append(slot[2])
        slot[1] += 16
        op.sem = slot[0]
        op.semval = slot[1]
        slot[2] = op
        if out_final:
            self.out_dmas.append(op)
        return op

    def make_ident(self, buf, n=128):
        self._add("pool", lambda e: e.memset(buf.ap, 0.0), [], [buf])
        return self._add("pool", lambda e: e.affine_select(
            out=buf.ap, in_=buf.ap, pattern=[[-1, n]], compare_op=ALU.not_equal,
            fill=1.0, base=0, channel_multiplier=1), [buf], [buf])

    def emit(self):
        nc = self.nc
        for e in self.ENGS:
            c = 0
            for op in self.ops[e]:
                if op.is_dma:
                    continue
                if op.needed:
                    c += 1
                op.count = c
        engmap = {"pe": "tensor", "act": "scalar", "dve": "vector", "pool": "gpsimd", "sp": "sync"}
        last_eng = "sp"

        def run(ename):
            def body(eng):
                waited = {}
                for op in self.ops[ename]:
                    for d in op.waits:
                        if d.is_dma:
                            sem, val = d.sem, d.semval
                        else:
                            sem, val = self.esem[d.eng], d.count
                        key = id(sem)
                        if waited.get(key, -1) >= val:
                            continue
                        waited[key] = val
                        eng.wait_ge(sem, val)
                    ins = op.fn(eng)
                    if op.is_dma:
                        ins.then_inc(op.sem, 16)
                    elif op.needed:
                        ins.then_inc(self.esem[ename], 1)
                if ename == last_eng:
                    for d in self.out_dmas:
                        eng.wait_ge(d.sem, d.semval)
            return body

        with nc.Block() as block:
            for ename in self.ENGS:
                getattr(block, engmap[ename])(run(ename))


class FreeList:
    def __init__(self, items):
        self.free = list(items)

    def get(self):
        if not self.free:
            raise RuntimeError("pool empty")
        return self.free.pop(0)

    def put(self, *items):
        for it in items:
            self.free.append(it)


D = 1024
DC = 512
NH = 8
TB = 512
NB = 4
NS = 16
L_SEQ = 2048
DIN = 7184
OFF_Q, OFF_K, OFF_V, OFF_Z, OFF_B, OFF_A, OFF_GA, OFF_GB = 1024, 2048, 3072, 4096, 5120, 5128, 5136, 6160
EPS = 1e-6
NMOD = 48
BIG = 30000.0
MUL, ADD, SUB, MAX = ALU.mult, ALU.add, ALU.subtract, ALU.max


class _Stop(Exception):
    pass


def build_program(n_blocks=NB, with_samples=True, stop=None):
    nc = bass.Bass("TRN2", target_bir_lowering=False)
    S = Sched(nc)
    try:
        _build_body(nc, S, n_blocks, with_samples, stop)
    except _Stop:
        pass
    S.emit()
    return nc


def _build_body(nc, S, n_blocks, with_samples, stop):
    dbg_out = nc.dram_tensor("dbg", [128, 512], F32, kind="ExternalOutput").ap() if stop else None

    def chk(name, ap, buf, rows=128, cols=512):
        if stop == name:
            import os as _os
            nsp = int(_os.environ.get("SPINCHK", "0"))
            if nsp:
                spb = S.sb("spinbuf", [128, 512])
                S.dve(lambda e: e.memset(spb.ap, 1.0), [], [spb])
                for _ in range(nsp):
                    S.dve(lambda e: e.tensor_scalar(out=spb.ap, in0=spb.ap, scalar1=1.0, scalar2=None, op0=MUL),
                          [spb, buf], [spb, buf])
            S.dma("pool", dbg_out[0:rows, 0:cols], ap, [buf], [], out_final=True)
            raise _Stop()

    def din(name, shape):
        return nc.dram_tensor(name, list(shape), F32, kind="ExternalInput").ap()

    def dout(name, shape):
        return nc.dram_tensor(name, list(shape), F32, kind="ExternalOutput").ap()

    x_p = din("x_p", [L_SEQ, D]); x_s = din("x_s", [NS, D]); c_all = din("c_all", [NS + 1, D])
    st_conf = din("st_conf", [NS, 30, DC]); st_qkv = din("st_qkv", [NS, 3, 3072])
    st_delta = din("st_delta", [NS, NH, 128, 128])
    w_ada = din("w_ada", [D, 6 * D + 128]); b_ada = din("b_ada", [6 * D]); norm1_w = din("norm1_w", [D])
    w_in = din("w_in", [D, DIN + 128]); conf_dw_w = din("conf_dw_w", [31, DC]); conf_dw_b = din("conf_dw_b", [DC])
    conf_ln_w = din("conf_ln_w", [DC]); conf_ln_b = din("conf_ln_b", [DC]); w_conf_out = din("w_conf_out", [DC, D + 128])
    gdn_conv_w = din("gdn_conv_w", [4, 3072]); a_log = din("a_log", [NH]); dt_bias = din("dt_bias", [NH])
    gdn_norm_w = din("gdn_norm_w", [128]); w_gdn_out = din("w_gdn_out", [D, D + 128]); w_o = din("w_o", [D, D + 128])
    norm2_w = din("norm2_w", [D]); w_ff1 = din("w_ff1", [D, 4 * D + 128]); w_ff2 = din("w_ff2", [4 * D, D + 128])
    final_norm_w = din("final_norm_w", [D])
    y_p = dout("y_p", [L_SEQ, D]); y_s = dout("y_s", [NS, D]); nconf_p = dout("nconf_p", [30, DC])
    nqkv_p = dout("nqkv_p", [3, 3072]); ndelta_p = dout("ndelta_p", [NH, 128, 128])
    nconf_s = dout("nconf_s", [NS, 30, DC]); nqkv_s = dout("nqkv_s", [NS, 3, 3072])
    ndelta_s = dout("ndelta_s", [NS, NH, 128, 128])

    def kview(w):
        return w.rearrange("(kc p) e -> p kc e", p=128)
    w_ada_v, w_in_v, w_co_v, w_go_v, w_o_v, w_f1_v, w_f2_v = (kview(w) for w in
                                                              (w_ada, w_in, w_conf_out, w_gdn_out, w_o, w_ff1, w_ff2))

    NW = 5
    wpool = FreeList([S.sb(f"w{i}", [128, 8, 128]) for i in range(NW)])
    NR, NP = 28, 20
    rpool = FreeList([S.sb(f"ru{i}", [128, 512]) for i in range(NR)])
    ppool = FreeList([S.sb(f"pu{i}", [128, 512]) for i in range(NP)])
    banks = [S.ps(f"bank{i}", [128, 512]) for i in range(8)]
    class _BankPool:
        def __init__(self, bs):
            self.free = list(bs)

        def get(self, w):
            if not self.free:
                raise RuntimeError("psum pool empty")
            bk = self.free.pop(0)
            v = Buf("pv", bk.ap[:, 0:w], root=bk)
            v.bank = bk
            return v

        def put(self, *vs):
            for v in vs:
                self.free.append(v.bank)

    class _Shim:
        def __init__(self, pool, w):
            self.pool = pool; self.w = w

        def get(self):
            return self.pool.get(self.w)

        def put(self, *vs):
            self.pool.put(*vs)
    _bp = _BankPool(banks)
    pfull = _Shim(_bp, 512); phalf = _Shim(_bp, 256); pq = _Shim(_bp, 128)
    NT_R, NT_P = 26, 19
    trpool = FreeList([S.sb(f"tr{i}", [128, 128]) for i in range(NT_R)])
    tppool = FreeList([S.sb(f"tp{i}", [128, 128]) for i in range(NT_P)])
    r2pool = FreeList([S.sb(f"r2{i}", [128, 256]) for i in range(16)])

    def load_w(src, nk=8, ncol=128):
        b = wpool.get()
        S.dma("sp", b.ap[:, 0:nk, 0:ncol].bitcast(F32R), src.bitcast(F32R), [], [b], precook=False)
        return b

    ident = S.sb("ident", [128, 128]); S.make_ident(ident)
    ones = S.sb("ones", [128, 128]); S.pool(lambda e: e.memset(ones.ap, 1.0), [], [ones])
    epsb = S.sb("epsb", [128, 1]); S.pool(lambda e: e.memset(epsb.ap, EPS), [], [epsb])
    chk("ident", ident.ap, ident, 128, 128)

    def mk_mask(name, wt, keep, fill, sign, cmp, fr, fc, fv):
        b = S.sb(name, [128, wt * 128])
        S.pool(lambda e: e.memset(b.ap, keep), [], [b])
        if wt > 1:
            v = b.ap.rearrange("p (u j) -> p u j", j=128); pat = [[0, wt], [-sign, 128]]
        else:
            v = b.ap; pat = [[-sign, 128]]
        S.pool(lambda e: e.affine_select(out=v, in_=v, pattern=pat, compare_op=cmp, fill=fill, base=0,
                                         channel_multiplier=sign), [b], [b])
        for u in range(wt):
            S.pool(lambda e, u=u: e.memset(b.ap[fr[0]:fr[1], u * 128 + fc[0]:u * 128 + fc[1]], fv), [b], [b])
        return b
    mL4 = mk_mask("mL4", 4, 0.0, BIG, 1, ALU.is_gt, (64, 128), (0, 64), BIG)
    mU4 = mk_mask("mU4", 4, 0.0, BIG, -1, ALU.is_ge, (0, 64), (64, 128), BIG)
    Mcum = mk_mask("Mcum", 1, 1.0, 0.0, -1, ALU.is_ge, (0, 64), (64, 128), 0.0)
    Maft = mk_mask("Maft", 1, 1.0, 0.0, 1, ALU.is_gt, (64, 128), (0, 64), 0.0)
    M32 = S.sb("M32", [128, 128])
    S.pool(lambda e: e.memset(M32.ap, 0.0), [], [M32])
    for q_ in range(4):
        S.pool(lambda e, q_=q_: e.memset(M32.ap[32 * q_:32 * q_ + 32, 32 * q_:32 * q_ + 32], 1.0), [M32], [M32])
    stage = S.sb("stage", [128, 128])
    S.dve(lambda e: e.memset(stage.ap, 0.0), [], [stage])
    r0 = 0
    VOFF = {}
    for nm, v, n in (("b_ada", b_ada, 48), ("n1", norm1_w, 8), ("n2", norm2_w, 8), ("nf", final_norm_w, 8),
                     ("dwb", conf_dw_b, 4), ("lnw", conf_ln_w, 4), ("lnb", conf_ln_b, 4), ("gnw", gdn_norm_w, 1)):
        S.dma("pool", stage.ap[r0:r0 + n, :], v.rearrange("(r p) -> r p", p=128), [], [stage])
        VOFF[nm] = r0
        r0 += n
    vecT = S.sb("vecT", [128, 96])
    t_ = pq.get()
    S.tr(t_.ap, stage.ap, ident.ap, [stage, ident], [t_])
    S.act(vecT.ap[:, 0:85], t_.ap[:, 0:85], AF.Copy, reads=[t_], writes=[vecT])
    pq.put(t_)
    S.dve(lambda e: e.tensor_scalar(out=vecT.ap[:, 85:93], in0=vecT.ap[:, VOFF["lnw"]:VOFF["lnw"] + 8], scalar1=0.5,
                                    scalar2=None, op0=MUL), [vecT], [vecT])
    chk("vecT", vecT.ap, vecT, 128, 96)

    def vcol(nm, i=0, off=0):
        c = VOFF[nm] + i + off
        return vecT.ap[:, c:c + 1]

    stg = tppool.get()
    S.dve(lambda e, stg=stg: e.memset(stg.ap, 0.0), [], [stg])
    S.dma("pool", stg.ap[0:124, :], conf_dw_w.rearrange("j (c p) -> (j c) p", p=128), [], [stg])
    dwT = S.sb("dwT", [128, 31, 4])
    t_ = pq.get()
    S.tr(t_.ap, stg.ap, ident.ap, [stg, ident], [t_])
    S.act(dwT.ap.rearrange("p j c -> p (j c)"), t_.ap[:, 0:124], AF.Copy, reads=[t_], writes=[dwT])
    pq.put(t_); tppool.put(stg)
    chk("dwT", dwT.ap.rearrange("p j c -> p (j c)"), dwT, 128, 124)
    stg = tppool.get()
    S.dve(lambda e, stg=stg: e.memset(stg.ap, 0.0), [], [stg])
    S.dma("pool", stg.ap[0:96, :], gdn_conv_w.rearrange("j (c p) -> (j c) p", p=128), [], [stg])
    gcT = S.sb("gcT", [128, 4, 24])
    t_ = pq.get()
    S.tr(t_.ap, stg.ap, ident.ap, [stg, ident], [t_])
    S.act(gcT.ap.rearrange("p j c -> p (j c)"), t_.ap[:, 0:96], AF.Copy, reads=[t_], writes=[gcT])
    pq.put(t_); tppool.put(stg)
    chk("gcT", gcT.ap.rearrange("p j c -> p (j c)"), gcT, 128, 96)
    hp8 = S.sb("hp8", [8, 4])
    S.dma("pool", hp8.ap[:, 0:1], a_log.rearrange("(h o) -> h o", o=1), [], [hp8])
    S.dma("pool", hp8.ap[:, 1:2], dt_bias.rearrange("(h o) -> h o", o=1), [], [hp8])
    S.act(hp8.ap[:, 2:3], hp8.ap[:, 0:1], AF.Exp, reads=[hp8], writes=[hp8])
    S.dve(lambda e: e.tensor_scalar(out=hp8.ap[:, 2:3], in0=hp8.ap[:, 2:3], scalar1=-1.0, scalar2=None, op0=MUL),
          [hp8], [hp8])
    hpb = S.sb("hpb", [128, 3, 8])
    S.dma("pool", hpb.ap[:, 0, :], a_log.partition_broadcast(128), [], [hpb])
    S.dma("pool", hpb.ap[:, 1, :], dt_bias.partition_broadcast(128), [], [hpb])
    S.act(hpb.ap[:, 2, :], hpb.ap[:, 0, :], AF.Exp, reads=[hpb], writes=[hpb])
    S.dve(lambda e: e.tensor_scalar(out=hpb.ap[:, 2, :], in0=hpb.ap[:, 2, :], scalar1=-1.0, scalar2=None, op0=MUL),
          [hpb], [hpb])

    chk("hpb", hpb.ap.rearrange("p a b -> p (a b)"), hpb, 128, 24)
    chk("hp8", hp8.ap, hp8, 8, 4)
    def tt(out, a, b, op, reads, writes, eng="dve"):
        return S.any(eng, lambda e: e.tensor_tensor(out=out, in0=a, in1=b, op=op), reads, writes)

    def ts(out, a, s1, op0, reads, writes, s2=None, op1=None, eng="dve"):
        if op1 is None:
            return S.any(eng, lambda e: e.tensor_scalar(out=out, in0=a, scalar1=s1, scalar2=None, op0=op0),
                         reads, writes)
        return S.any(eng, lambda e: e.tensor_scalar(out=out, in0=a, scalar1=s1, scalar2=s2, op0=op0, op1=op1),
                     reads, writes)

    def stt(out, a, sc, b, op0, op1, reads, writes):
        return S.dve(lambda e: e.scalar_tensor_tensor(out=out, in0=a, scalar=sc, in1=b, op0=op0, op1=op1),
                     reads, writes)

    def silu_from_psum(pb, out_ap, out_buf, n=512, pre_scale=1.0):
        xh = ppool.get(); th = ppool.get()
        S.act(xh.ap[:, 0:n], pb.ap[:, 0:n], AF.Identity, scale=0.5 * pre_scale, reads=[pb], writes=[xh])
        S.act(th.ap[:, 0:n], xh.ap[:, 0:n], AF.Tanh, reads=[xh], writes=[th])
        stt(out_ap, th.ap[:, 0:n], 1.0, xh.ap[:, 0:n], ADD, MUL, [th, xh], [out_buf])
        ppool.put(xh, th)

    def rstd_from_chunks(chunks, denom, eps=EPS):
        pb = pfull.get()
        n = len(chunks)
        for i, (ap, buf) in enumerate(chunks):
            sq = rpool.get()
            tt(sq.r, ap, ap, MUL, [buf], [sq], eng="pool")
            S.mm(pb.ap, ones.r, sq.r, i == 0, i == n - 1, [ones, sq], [pb])
            rpool.put(sq)
        ln = ppool.get()
        S.act(ln.ap, pb.ap, AF.Ln, scale=1.0 / denom, bias=epsb.ap, reads=[pb, epsb], writes=[ln])
        pfull.put(pb)
        S.act(ln.ap, ln.ap, AF.Exp, scale=-0.5, reads=[ln], writes=[ln])
        return ln

    NC18 = 18
    c0 = ppool.get(); c1 = ppool.get()
    S.dve(lambda e: e.memset(c0.ap, 0.0), [], [c0])
    S.dve(lambda e: e.memset(c1.ap, 0.0), [], [c1])
    S.dma("sp", c0.ap[0:NS + 1, :], c_all[:, 0:512], [], [c0])
    S.dma("sp", c1.ap[0:NS + 1, :], c_all[:, 512:1024], [], [c1])
    scT = S.sb("scT", [128, 8, NC18]); sch = S.sb("sch", [128, 8, NC18]); sct = S.sb("sct", [128, 8, NC18])
    S.dve(lambda e: e.memset(sch.ap.rearrange("p a b -> p (a b)"), 0.0), [], [sch])
    for kc in range(8):
        src = c0 if kc < 4 else c1
        t_ = pq.get()
        S.tr(t_.ap, src.ap[:, (kc % 4) * 128:(kc % 4 + 1) * 128], ident.ap, [src, ident], [t_])
        S.act(sch.ap[:, kc, 0:NS + 1], t_.ap[:, 0:NS + 1], AF.Identity, scale=0.5, reads=[t_], writes=[sch])
        pq.put(t_)
    ppool.put(c0, c1)
    chk("sch", sch.ap.rearrange("p a b -> p (a b)"), sch, 128, 8 * NC18)
    S.act(sct.ap, sch.ap, AF.Tanh, reads=[sch], writes=[sct])
    chk("sct", sct.ap.rearrange("p a b -> p (a b)"), sct, 128, 8 * NC18)
    stt(scT.r, sct.ap, 1.0, sch.ap, ADD, MUL, [sct, sch], [scT])
    chk("scT", scT.ap.rearrange("p a b -> p (a b)"), scT, 128, 8 * NC18)
    modT = S.sb("modT", [128, NC18, 48])
    modS = S.sb("modS", [128, 48, NS])
    def mod_tile(e_):
        wt = load_w(w_ada_v[:, :, e_ * 128:(e_ + 1) * 128])
        t_ = pq.get()
        for kc in range(8):
            S.mm(t_.ap[:, 0:NC18], wt.r[:, kc, :], scT.r[:, kc, :], kc == 0, kc == 7, [wt, scT], [t_])
        S.act(modT.ap[:, :, e_], t_.ap[:, 0:NC18], AF.Identity, bias=vcol("b_ada", e_), reads=[t_, vecT], writes=[modT])
        S.act(modS.ap[:, e_, :], t_.ap[:, 1:NS + 1], AF.Identity, bias=vcol("b_ada", e_), reads=[t_, vecT], writes=[modS])
        pq.put(t_)
        wpool.put(wt)
    for e_ in range(16):
        mod_tile(e_)
    dp = S.sb("dp", [128, 24])
    modP = S.sb("modP", [128, 48])
    S.dve(lambda e: e.tensor_copy(out=modP.ap[:, 0:16], in_=modT.ap[:, 0, 0:16]), [modT], [modP])
    stt(dp.ap[:, 0:8], modP.ap[:, 8:16], 1.0, vecT.ap[:, VOFF["n1"]:VOFF["n1"] + 8], ADD, MUL, [modP, vecT], [dp])

    def modB_gen():
        for e_ in range(16, NMOD):
            mod_tile(e_)
            yield
    modB = modB_gen()

    def finish_modB():
        for _ in modB:
            pass
        S.dve(lambda e: e.tensor_copy(out=modP.ap[:, 16:48], in_=modT.ap[:, 0, 16:48]), [modT], [modP])
        ts(dp.ap[:, 8:16], modP.ap[:, 16:24], 0.5, MUL, [modP], [dp])
        stt(dp.ap[:, 16:24], modP.ap[:, 32:40], 1.0, vecT.ap[:, VOFF["n2"]:VOFF["n2"] + 8], ADD, MUL, [modP, vecT], [dp])
    A1p = lambda c: dp.ap[:, c:c + 1]
    B1p = lambda c: modP.ap[:, c:c + 1]
    HG1p = lambda c: dp.ap[:, 8 + c:9 + c]
    A2p = lambda c: dp.ap[:, 16 + c:17 + c]
    B2p = lambda c: modP.ap[:, 24 + c:25 + c]
    G2p = lambda c: modP.ap[:, 40 + c:41 + c]
    chk("modT", modT.ap.rearrange("p a b -> p (a b)")[:, 0:512], modT, 128, 512)

    chk("dp", dp.ap, dp, 128, 24)
    gluext = S.sb("gluext", [128, 4, 30 + TB])
    for c in range(4):
        S.dve(lambda e, c=c: e.memset(gluext.ap[:, c, 0:30], 0.0), [], [gluext])
    chk("init0", gluext.ap[:, 0, 0:128], gluext, 128, 128)
    qkvext = [S.sb(f"qkvext{i}", [128, 3, 3 + TB]) for i in range(1)]
    qkvhist = S.sb("qkvhist", [128, 24, 3])
    S.dve(lambda e: e.memset(qkvhist.ap, 0.0), [], [qkvhist])
    chk("init1", qkvhist.ap.rearrange("p a b -> p (a b)"), qkvhist, 128, 72)
    Sst = [S.sb(f"Sst{h}", [128, 128]) for h in range(NH)]
    for h in range(NH):
        S.dve(lambda e, h=h: e.memset(Sst[h].ap, 0.0), [], [Sst[h]])
    chk("init2", Sst[7].ap, Sst[7], 128, 128)
    vnew = S.sb("vnew", [128, 128])
    S.dve(lambda e: e.memset(vnew.ap, 0.0), [], [vnew])
    KdA = [S.sb(f"KdA{i}", [128, 128]) for i in range(4)]
    KdB = [S.sb(f"KdB{i}", [128, 128]) for i in range(4)]
    for i in range(4):
        S.dve(lambda e, i=i: e.memset(KdA[i].ap, 0.0), [], [KdA[i]])
        S.dve(lambda e, i=i: e.memset(KdB[i].ap, 0.0), [], [KdB[i]])
    colT = [S.sb(f"colT{u}", [128, 48]) for u in range(4)]

    chk("init", vnew.ap, vnew, 128, 128)
    def samp_mm(wt, nk, ncol, samp, c0=0):
        srhs, dst_ap, dst_buf, mode = samp
        t_ = pq.get()
        for kc in range(nk):
            S.mm(t_.ap[0:ncol, 0:NS], wt.r[:, kc, c0:c0 + ncol], srhs.r[:, kc, :], kc == 0, kc == nk - 1, [wt, srhs], [t_])
        if mode == "copy":
            S.act(dst_ap, t_.ap[0:ncol, 0:NS], AF.Copy, reads=[t_], writes=[dst_buf])
        elif mode == "tanh":
            S.act(dst_ap, t_.ap[0:ncol, 0:NS], AF.Tanh, scale=0.5, reads=[t_], writes=[dst_buf])
        elif mode == "relu":
            S.act(dst_ap, t_.ap[0:ncol, 0:NS], AF.Relu, reads=[t_], writes=[dst_buf])
        pq.put(t_)

    def proj(wcols, rhs_bufs, out_ap, out_buf, nk=8, wview=w_in_v, ncol=128, rsl=slice(0, TB), samp=None):
        wt = load_w(wview[:, 0:nk, wcols:wcols + ncol], nk=nk, ncol=ncol)
        for kc in range(nk):
            S.mm(out_ap, wt.r[:, kc, 0:ncol], rhs_bufs[kc].r[:, rsl], kc == 0, kc == nk - 1, [wt, rhs_bufs[kc]], [out_buf])
        if samp is not None:
            samp_mm(wt, nk, ncol, samp)
        wpool.put(wt)

    xsT = S.sb("xsT", [128, 8, NS]); hsT = S.sb("hsT", [128, 8, NS]); sA = S.sb("sA", [128, 8, NS])
    a1s = S.sb("a1s", [128, 4, NS]); aTs = S.sb("aTs", [128, 4, NS]); gS = S.sb("gS", [128, 8, NS])
    mgS = S.sb("mgS", [128, 8, NS]); qkvS = S.sb("qkvS", [128, 24, NS]); qkvC = S.sb("qkvC", [128, 24, NS])
    zS = S.sb("zS", [128, 8, NS]); oTs = S.sb("oTs", [128, 8, NS]); fS = S.sb("fS", [128, 32, NS])
    rawS = S.sb("rawS", [128, 8, NS]); ba8 = S.sb("ba8", [8, 2, NS])

    def flat(buf, a=None, b=None):
        v = buf.ap if a is None else buf.ap[:, a:b, :]
        return v.rearrange("p a s -> p (a s)")

    def s_rstd(src, nch, denom):
        sq = tppool.get()
        n = nch * NS
        tt(sq.ap[:, 0:n], flat(src, 0, nch), flat(src, 0, nch), MUL, [src], [sq])
        t_ = pq.get()
        for kc in range(nch):
            S.mm(t_.ap[:, 0:NS], ones.ap, sq.ap[:, kc * NS:(kc + 1) * NS], kc == 0, kc == nch - 1, [ones, sq], [t_])
        S.act(sq.ap[:, 0:NS], t_.ap[:, 0:NS], AF.Ln, scale=1.0 / denom, bias=epsb.ap, reads=[t_, epsb], writes=[sq])
        pq.put(t_)
        S.act(sq.ap[:, 0:NS], sq.ap[:, 0:NS], AF.Exp, scale=-0.5, reads=[sq], writes=[sq])
        return sq

    def s_modnorm(dst, vec_nm, sc_off, sh_off):
        rs_ = s_rstd(xsT, 8, float(D))
        t1 = tppool.get(); t2 = tppool.get()
        for kc in range(8):
            ts(t1.ap[:, 0:NS], modS.ap[:, sc_off + kc, :], 1.0, ADD, [modS], [t1])
            ts(t1.ap[:, 0:NS], t1.ap[:, 0:NS], vcol(vec_nm, kc), MUL, [t1, vecT], [t1])
            tt(t2.ap[:, 0:NS], xsT.ap[:, kc, :], rs_.ap[:, 0:NS], MUL, [xsT, rs_], [t2])
            tt(t2.ap[:, 0:NS], t2.ap[:, 0:NS], t1.ap[:, 0:NS], MUL, [t2, t1], [t2])
            tt(dst.r[:, kc, :], t2.ap[:, 0:NS], modS.ap[:, sh_off + kc, :], ADD, [t2, modS], [dst])
        tppool.put(rs_, t1, t2)

    def s_to_tm(src_fn, nch, dram_fn, reads):
        for g0 in range(0, nch, 4):
            pb = pfull.get()
            ng = min(4, nch - g0)
            for cc in range(ng):
                S.tr(pb.ap[0:NS, cc * 128:(cc + 1) * 128], src_fn(g0 + cc), ident.ap, reads + [ident], [pb])
            o_ = ppool.get()
            S.act(o_.ap[0:NS, 0:ng * 128], pb.ap[0:NS, 0:ng * 128], AF.Copy, reads=[pb], writes=[o_])
            pfull.put(pb)
            S.dma("pool", dram_fn(g0, ng), o_.ap[0:NS, 0:ng * 128], [o_], [], out_final=True)
            ppool.put(o_)

    for b in range(n_blocks):
        t0 = b * TB
        last = (b == NB - 1)
        xtm = [[ppool.get(), ppool.get()] for _ in range(4)]
        for s_ in range(4):
            for hf in range(2):
                S.dma("pool", xtm[s_][hf].ap, x_p[t0 + s_ * 128:t0 + (s_ + 1) * 128, hf * 512:(hf + 1) * 512],
                      [], [xtm[s_][hf]])
        xT = [ppool.get() for _ in range(8)]
        for c in range(8):
            pb = pfull.get()
            for s_ in range(4):
                src = xtm[s_][c // 4]
                S.tr(pb.ap[:, s_ * 128:(s_ + 1) * 128], src.ap[:, (c % 4) * 128:(c % 4 + 1) * 128], ident.ap,
                     [src, ident], [pb])
            if c % 2 == 0:
                S.act(xT[c].ap, pb.ap, AF.Copy, reads=[pb], writes=[xT[c]])
            else:
                S.dve(lambda e, c=c, pb=pb, xT=xT: e.tensor_copy(out=xT[c].ap, in_=pb.ap), [pb], [xT[c]])
            pfull.put(pb)
        for s_ in range(4):
            ppool.put(*xtm[s_])
        chk("xT", xT[1].ap, xT[1])
        rs = rstd_from_chunks([(xT[c].ap, xT[c]) for c in range(8)], float(D))
        chk("rs", rs.ap, rs)
        hT = [rpool.get() for _ in range(8)]
        for c in range(8):
            tt(hT[c].r, xT[c].ap, rs.ap, MUL, [xT[c], rs], [hT[c]])
            ts(hT[c].r, hT[c].ap, A1p(c), MUL, [hT[c], dp], [hT[c]])
            ts(hT[c].r, hT[c].ap, B1p(c), ADD, [hT[c], modP], [hT[c]])
            import os as _os
            if c == 1 and _os.environ.get("SPIN"):
                for _ in range(int(_os.environ["SPIN"])):
                    ts(hT[0].r, hT[0].ap, 1.0, MUL, [hT[0]], [hT[0]])
            chk(f"hT{c}", hT[c].ap, hT[c])
        ppool.put(rs)

        smp = (b == 0 and with_samples)
        if smp:
            xs0 = ppool.get(); xs1 = ppool.get()
            for u_, hf in ((xs0, 0), (xs1, 1)):
                S.dve(lambda e, u_=u_: e.memset(u_.ap, 0.0), [], [u_])
                S.dma("pool", u_.ap[0:NS, :], x_s[:, hf * 512:(hf + 1) * 512], [], [u_])
            for kc in range(8):
                src = xs0 if kc < 4 else xs1
                t_ = pq.get()
                S.tr(t_.ap, src.ap[:, (kc % 4) * 128:(kc % 4 + 1) * 128], ident.ap, [src, ident], [t_])
                S.act(xsT.ap[:, kc, :], t_.ap[:, 0:NS], AF.Copy, reads=[t_], writes=[xsT])
                pq.put(t_)
            ppool.put(xs0, xs1)
            s_modnorm(hsT, "n1", 8, 0)
        chk("hT", hT[0].ap, hT[0])
        for c in range(4):
            pv_ = pfull.get(); pg_ = pfull.get()
            proj(c * 128, hT, pv_.ap, pv_, samp=(hsT, sA.ap[:, c, :], sA, "copy") if smp else None)
            proj(DC + c * 128, hT, pg_.ap, pg_, samp=(hsT, sA.ap[:, 4 + c, :], sA, "tanh") if smp else None)
            tg = ppool.get(); vh = ppool.get()
            S.act(tg.ap, pg_.ap, AF.Tanh, scale=0.5, reads=[pg_], writes=[tg])
            S.act(vh.ap, pv_.ap, AF.Identity, scale=0.5, reads=[pv_], writes=[vh])
            pfull.put(pv_, pg_)
            stt(gluext.r[:, c, 30:30 + TB], tg.ap, 1.0, vh.ap, ADD, MUL, [tg, vh], [gluext])
            ppool.put(tg, vh)
        if smp:
            exu = [ppool.get() for _ in range(4)]
            exv = [u_.ap[:, 0:31 * NS].rearrange("p (j s) -> p j s", s=NS) for u_ in exu]
            for g_ in range(4):
                stc = ppool.get()
                S.dve(lambda e, stc=stc: e.memset(stc.ap, 0.0), [], [stc])
                S.dma("pool", stc.ap[0:120, :], st_conf[4 * g_:4 * g_ + 4].rearrange("s j c -> (s j) c"), [], [stc])
                for c in range(4):
                    t_ = pq.get()
                    S.tr(t_.ap, stc.ap[:, c * 128:(c + 1) * 128], ident.ap, [stc, ident], [t_])
                    S.act(exv[c][:, 0:30, 4 * g_:4 * g_ + 4], t_.ap[:, 0:120].rearrange("p (s j) -> p j s", j=30),
                          AF.Copy, reads=[t_], writes=[exu[c]])
                    pq.put(t_)
                ppool.put(stc)
            for c in range(4):
                ts(sA.ap[:, c, :], sA.ap[:, c, :], 0.5, MUL, [sA], [sA])
                stt(exv[c][:, 30, :], sA.ap[:, 4 + c, :], 1.0, sA.ap[:, c, :], ADD, MUL, [sA], [exu[c]])
            S.dma("pool", nconf_s[:, 0:29, :], st_conf[:, 1:30, :], [], [], out_final=True)
            s_to_tm(lambda c: exv[c][:, 30, :], 4, lambda g0, ng: nconf_s[:, 29, g0 * 128:(g0 + ng) * 128], exu)
            acc = tppool.get()
            for c in range(4):
                ts(acc.ap[:, 0:NS], exv[c][:, 0, :], dwT.ap[:, 0, c:c + 1], MUL, [exu[c], dwT], [acc])
                for j in range(1, 31):
                    stt(acc.ap[:, 0:NS], exv[c][:, j, :], dwT.ap[:, j, c:c + 1], acc.ap[:, 0:NS], MUL, ADD,
                        [exu[c], dwT, acc], [acc])
                ts(a1s.ap[:, c, :], acc.ap[:, 0:NS], vcol("dwb", c), ADD, [acc, vecT], [a1s])
            tppool.put(acc); ppool.put(*exu)
            t_ = pq.get()
            for c in range(4):
                S.mm(t_.ap[:, 0:NS], ones.ap, a1s.ap[:, c, :], c == 0, c == 3, [ones, a1s], [t_])
            mean_ = tppool.get()
            S.act(mean_.ap[:, 0:NS], t_.ap[:, 0:NS], AF.Identity, scale=1.0 / DC, reads=[t_], writes=[mean_])
            pq.put(t_)
            for c in range(4):
                tt(a1s.ap[:, c, :], a1s.ap[:, c, :], mean_.ap[:, 0:NS], SUB, [a1s, mean_], [a1s])
            rs_ = s_rstd(a1s, 4, float(DC))
            for c in range(4):
                tt(a1s.ap[:, c, :], a1s.ap[:, c, :], rs_.ap[:, 0:NS], MUL, [a1s, rs_], [a1s])
                ts(a1s.ap[:, c, :], a1s.ap[:, c, :], vecT.ap[:, 85 + c:86 + c], MUL, [a1s, vecT], [a1s])
                ts(a1s.ap[:, c, :], a1s.ap[:, c, :], vecT.ap[:, 89 + c:90 + c], ADD, [a1s, vecT], [a1s])
            S.act(mean_.ap[:, 0:4 * NS], flat(a1s), AF.Tanh, reads=[a1s], writes=[mean_])
            stt(flat(aTs).bitcast(F32R), mean_.ap[:, 0:4 * NS], 1.0, flat(a1s), ADD, MUL, [mean_, a1s], [aTs])
            tppool.put(mean_, rs_)
        if last:
            pb = pfull.get()
            for c in range(4):
                S.tr(pb.ap[0:30, c * 128:(c + 1) * 128], gluext.ap[:, c, TB:TB + 30], ident.ap, [gluext, ident], [pb])
            o_ = ppool.get()
            S.act(o_.ap[0:30, :], pb.ap[0:30, :], AF.Copy, reads=[pb], writes=[o_])
            pfull.put(pb)
            S.dma("pool", nconf_p, o_.ap[0:30, :], [o_], [], out_final=True)
            ppool.put(o_)
        a1 = [rpool.get() for _ in range(4)]
        for c in range(4):
            dgu = [rpool.get() for _ in range(8)]
            for j in range(31):
                S.pool(lambda e, c=c, j=j, dgu=dgu: e.tensor_scalar(
                    out=dgu[j // 4].r[:, (j % 4) * 128:(j % 4 + 1) * 128], in0=ident.ap, scalar1=dwT.ap[:, j, c:c + 1],
                    scalar2=1.0, op0=MUL, op1=MUL), [ident, dwT], [dgu[j // 4]])
            pc = pfull.get()
            for j in range(31):
                S.mm(pc.ap, dgu[j // 4].r[:, (j % 4) * 128:(j % 4 + 1) * 128], gluext.r[:, c, j:j + TB], j == 0, j == 30,
                     [dgu[j // 4], gluext], [pc])
            rpool.put(*dgu)
            S.act(a1[c].r, pc.ap, AF.Identity, bias=vcol("dwb", c), reads=[pc, vecT], writes=[a1[c]])
            pfull.put(pc)
        for c in range(4):
            S.pool(lambda e, c=c: e.tensor_copy(out=gluext.r[:, c, 0:30], in_=gluext.ap[:, c, TB:TB + 30]),
                   [gluext], [gluext])
        pm = pfull.get()
        for c in range(4):
            S.mm(pm.ap, ones.r, a1[c].r, c == 0, c == 3, [ones, a1[c]], [pm])
        mean = ppool.get()
        S.act(mean.ap, pm.ap, AF.Identity, scale=1.0 / DC, reads=[pm], writes=[mean])
        pfull.put(pm)
        p2 = pfull.get()
        for c in range(4):
            sq = rpool.get()
            S.act(sq.r, a1[c].ap, AF.Square, reads=[a1[c]], writes=[sq])
            S.mm(p2.ap, ones.r, sq.r, c == 0, c == 3, [ones, sq], [p2])
            rpool.put(sq)
        msq = ppool.get(); var = ppool.get()
        tt(msq.ap, mean.ap, mean.ap, MUL, [mean], [msq])
        stt(var.ap, p2.ap, 1.0 / DC, msq.ap, MUL, SUB, [p2, msq], [var])
        pfull.put(p2); ppool.put(msq)
        S.act(var.ap, var.ap, AF.Ln, bias=epsb.ap, reads=[var, epsb], writes=[var])
        S.act(var.ap, var.ap, AF.Exp, scale=-0.5, reads=[var], writes=[var])
        aT = [rpool.get() for _ in range(4)]
        for c in range(4):
            d_ = ppool.get(); xh = ppool.get(); th = ppool.get()
            tt(d_.ap, a1[c].ap, mean.ap, SUB, [a1[c], mean], [d_])
            tt(d_.ap, d_.ap, var.ap, MUL, [d_, var], [d_])
            ts(xh.ap, d_.ap, vecT.ap[:, 85 + c:86 + c], MUL, [d_, vecT], [xh])
            ts(xh.ap, xh.ap, vecT.ap[:, 89 + c:90 + c], ADD, [xh, vecT], [xh])
            S.act(th.ap, xh.ap, AF.Tanh, reads=[xh], writes=[th])
            stt(aT[c].r, th.ap, 1.0, xh.ap, ADD, MUL, [th, xh], [aT[c]])
            ppool.put(d_, xh, th)
        rpool.put(*a1)
        ppool.put(mean, var)
        mg = [rpool.get() for _ in range(8)]
        for e_ in range(8):
            pga = pfull.get()
            proj(OFF_GA + e_ * 128, hT, pga.ap, pga, samp=(hsT, gS.ap[:, e_, :], gS, "tanh") if smp else None)
            tga = ppool.get()
            S.act(tga.ap, pga.ap, AF.Tanh, scale=0.5, reads=[pga], writes=[tga])
            pfull.put(pga)
            pya = pfull.get()
            proj(e_ * 128, aT, pya.ap, pya, nk=4, wview=w_co_v,
                 samp=(aTs, rawS.ap[:, e_, :], rawS, "copy") if smp else None)
            if smp:
                stt(mgS.r[:, e_, :], gS.ap[:, e_, :], 1.0, rawS.ap[:, e_, :], ADD, MUL, [gS, rawS], [mgS])
            stt(mg[e_].r, tga.ap, 1.0, pya.ap, ADD, MUL, [tga, pya], [mg[e_]])
            pfull.put(pya); ppool.put(tga)
        rpool.put(*aT)
        chk("mgA", mg[0].ap, mg[0])

        wba = load_w(w_in_v[:, :, OFF_B:OFF_B + 16], nk=8, ncol=16)
        pa_ = pfull.get()
        for kc in range(8):
            S.mm(pa_.ap[0:8, :], wba.r[:, kc, 8:16], hT[kc].r, kc == 0, kc == 7, [wba, hT[kc]], [pa_])
        xg = ppool.get(); ax = ppool.get(); Grow = ppool.get(); rmask = ppool.get()
        S.pool(lambda e, rmask=rmask: e.memset(rmask.ap[0:8, :], 1.0), [], [rmask])
        for i_ in range(8):
            S.pool(lambda e, rmask=rmask, i_=i_: e.memset(rmask.ap[0:8, i_ * 64:i_ * 64 + 1], 0.0), [rmask], [rmask])
        S.act(xg.ap[0:8, :], pa_.ap[0:8, :], AF.Identity, bias=hp8.ap[:, 1:2], reads=[pa_, hp8], writes=[xg])
        pfull.put(pa_)
        stt(ax.ap[0:8, :], xg.ap[0:8, :], -1.0, xg.ap[0:8, :], MUL, MAX, [xg], [ax])
        S.act(ax.ap[0:8, :], ax.ap[0:8, :], AF.Exp, scale=-1.0, reads=[ax], writes=[ax])
        S.act(ax.ap[0:8, :], ax.ap[0:8, :], AF.Ln, bias=1.0, reads=[ax], writes=[ax])
        stt(xg.ap[0:8, :], xg.ap[0:8, :], 0.0, ax.ap[0:8, :], MAX, ADD, [xg, ax], [xg])
        ts(xg.ap[0:8, :], xg.ap[0:8, :], hp8.ap[:, 2:3], MUL, [xg, hp8], [xg])
        S.dve(lambda e, xg=xg, Grow=Grow, rmask=rmask: e.tensor_tensor_scan(out=Grow.ap[0:8, :], data0=rmask.ap[0:8, :], data1=xg.ap[0:8, :],
                                                               initial=0.0, op0=MUL, op1=ADD), [rmask, xg], [Grow])
        ppool.put(xg, ax, rmask)
        if smp:
            t_ = pq.get()
            for kc in range(8):
                S.mm(t_.ap[0:8, 0:NS], wba.r[:, kc, 0:8], hsT.r[:, kc, :], kc == 0, kc == 7, [wba, hsT], [t_])
            S.act(ba8.ap[:, 0, :], t_.ap[0:8, 0:NS], AF.Tanh, scale=0.5, reads=[t_], writes=[ba8])
            pq.put(t_)
            t_ = pq.get()
            for kc in range(8):
                S.mm(t_.ap[0:8, 0:NS], wba.r[:, kc, 8:16], hsT.r[:, kc, :], kc == 0, kc == 7, [wba, hsT], [t_])
            S.act(ba8.ap[:, 1, :], t_.ap[0:8, 0:NS], AF.Identity, bias=hp8.ap[:, 1:2], reads=[t_, hp8], writes=[ba8])
            pq.put(t_)
            ts(ba8.ap[:, 0, :], ba8.ap[:, 0, :], 0.5, MUL, [ba8], [ba8], s2=0.5, op1=ADD)
            sp_ = tppool.get()
            stt(sp_.ap[0:8, 0:NS], ba8.ap[:, 1, :], -1.0, ba8.ap[:, 1, :], MUL, MAX, [ba8], [sp_])
            S.act(sp_.ap[0:8, 0:NS], sp_.ap[0:8, 0:NS], AF.Exp, scale=-1.0, reads=[sp_], writes=[sp_])
            S.act(sp_.ap[0:8, 0:NS], sp_.ap[0:8, 0:NS], AF.Ln, bias=1.0, reads=[sp_], writes=[sp_])
            stt(ba8.ap[:, 1, :], ba8.ap[:, 1, :], 0.0, sp_.ap[0:8, 0:NS], MAX, ADD, [ba8, sp_], [ba8])
            ts(ba8.ap[:, 1, :], ba8.ap[:, 1, :], hp8.ap[:, 2:3], MUL, [ba8, hp8], [ba8])
            tppool.put(sp_)
        for u in range(4):
            us = slice(u * 128, (u + 1) * 128)
            t_ = pq.get()
            for kc in range(8):
                S.mm(t_.ap[:, 0:16], hT[kc].r[:, us], wba.r[:, kc, 0:16], kc == 0, kc == 7, [wba, hT[kc]], [t_])
            ct = colT[u]
            S.act(ct.ap[:, 0:8], t_.ap[:, 0:8], AF.Tanh, scale=0.5, reads=[t_], writes=[ct])
            ts(ct.ap[:, 0:8], ct.ap[:, 0:8], 0.5, MUL, [ct], [ct], s2=0.5, op1=ADD)
            tt(ct.ap[:, 40:48], t_.ap[:, 8:16], hpb.ap[:, 1, :], ADD, [t_, hpb], [ct])
            pq.put(t_)
            stt(ct.ap[:, 32:40], ct.ap[:, 40:48], -1.0, ct.ap[:, 40:48], MUL, MAX, [ct], [ct])
            S.act(ct.ap[:, 32:40], ct.ap[:, 32:40], AF.Exp, scale=-1.0, reads=[ct], writes=[ct])
            S.act(ct.ap[:, 32:40], ct.ap[:, 32:40], AF.Ln, bias=1.0, reads=[ct], writes=[ct])
            stt(ct.ap[:, 40:48], ct.ap[:, 40:48], 0.0, ct.ap[:, 32:40], MAX, ADD, [ct], [ct])
            tt(ct.ap[:, 40:48], ct.ap[:, 40:48], hpb.ap[:, 2, :], MUL, [ct, hpb], [ct])
            t2 = pq.get()
            S.mm(t2.ap[:, 0:8], Mcum.ap, ct.ap[:, 40:48], True, True, [Mcum, ct], [t2])
            S.mm(t2.ap[:, 8:16], Maft.ap, ct.ap[:, 40:48], True, True, [Maft, ct], [t2])
            S.act(ct.ap[:, 8:16], t2.ap[:, 0:8], AF.Copy, reads=[t2], writes=[ct])
            S.act(ct.ap[:, 16:24], t2.ap[:, 0:8], AF.Identity, scale=-1.0, reads=[t2], writes=[ct])
            S.act(ct.ap[:, 24:32], t2.ap[:, 0:8], AF.Exp, reads=[t2], writes=[ct])
            S.act(ct.ap[:, 32:40], t2.ap[:, 8:16], AF.Exp, reads=[t2], writes=[ct])
            pq.put(t2)
            tt(ct.ap[:, 24:32], ct.ap[:, 24:32], ct.ap[:, 0:8], MUL, [ct], [ct])
        wpool.put(wba)
        chk("colT", colT[0].ap, colT[0], 128, 48)

        oT = [rpool.get() for _ in range(8)]
        def front_gen(h, out):
            ext = qkvext[0]
            for i in range(3):
                ch = i * 8 + h
                S.pool(lambda e, ext=ext, i=i, ch=ch: e.tensor_copy(out=ext.r[:, i, 0:3], in_=qkvhist.ap[:, ch, :]),
                       [qkvhist], [ext])
                pb = pfull.get()
                proj(OFF_Q + ch * 128, hT, pb.ap, pb, samp=(hsT, qkvS.ap[:, ch, :], qkvS, "copy") if smp else None)
                S.act(ext.r[:, i, 3:3 + TB], pb.ap, AF.Copy, reads=[pb], writes=[ext])
                pfull.put(pb)
                S.pool(lambda e, ext=ext, i=i, ch=ch: e.tensor_copy(out=qkvhist.r[:, ch, :], in_=ext.ap[:, i, TB:TB + 3]),
                       [ext], [qkvhist])
                yield
            if last:
                t_ = pq.get(); o_ = tppool.get()
                for i in range(3):
                    S.tr(t_.ap[0:3, :], ext.ap[:, i, TB:TB + 3], ident.ap, [ext, ident], [t_])
                    S.act(o_.ap[0:3, :], t_.ap[0:3, :], AF.Copy, reads=[t_], writes=[o_])
                    S.dma("pool", nqkv_p[:, (i * 8 + h) * 128:(i * 8 + h + 1) * 128], o_.ap[0:3, :], [o_], [],
                          out_final=True)
                pq.put(t_); tppool.put(o_)
            dgu = [rpool.get() for _ in range(3)]
            sil = []
            for i in range(3):
                ch = i * 8 + h
                for j in range(4):
                    S.pool(lambda e, i=i, j=j, ch=ch, dgu=dgu: e.tensor_scalar(
                        out=dgu[i].r[:, j * 128:(j + 1) * 128], in0=ident.ap, scalar1=gcT.ap[:, j, ch:ch + 1],
                        scalar2=1.0, op0=MUL, op1=MUL), [ident, gcT], [dgu[i]])
                pc = pfull.get()
                for j in range(4):
                    S.mm(pc.ap, dgu[i].r[:, j * 128:(j + 1) * 128], ext.r[:, i, j:j + TB], j == 0, j == 3, [dgu[i], ext], [pc])
                o_ = ppool.get()
                silu_from_psum(pc, o_.ap, o_)
                pfull.put(pc)
                sil.append(o_)
                yield
            rpool.put(*dgu)
            qs, ks, vs = sil
            pz = pfull.get()
            proj(OFF_Z + h * 128, hT, pz.ap, pz, samp=(hsT, zS.ap[:, h, :], zS, "copy") if smp else None)
            zs = ppool.get()
            silu_from_psum(pz, zs.ap, zs)
            pfull.put(pz)
            out.extend([qs, ks, vs, zs])

        def run_gens(gens):
            gens = list(gens)
            while gens:
                for g_ in list(gens):
                    try:
                        next(g_)
                    except StopIteration:
                        gens.remove(g_)

        def mid_gen(h, sil, out):
            qs, ks, vs, zs = sil
            pg = pfull.get()
            gsel = ppool.get()
            S.dve(lambda e, gsel=gsel: e.memset(gsel.ap, 0.0), [], [gsel])
            ts(gsel.ap[0:8, :], Grow.ap[0:8, :], ident.ap[0:8, h:h + 1], MUL, [Grow, ident], [gsel])
            S.mm(pg.ap, ones.ap, gsel.ap, True, True, [ones, gsel], [pg])
            ppool.put(gsel)
            gam = ppool.get(); argL = ppool.get(); argU = ppool.get()
            S.act(gam.ap, pg.ap, AF.Exp, reads=[pg], writes=[gam])
            tt(argL.ap, pg.ap, mL4.ap, ADD, [pg, mL4], [argL])
            tt(argU.ap, pg.ap, mU4.ap, SUB, [pg, mU4], [argU])
            pfull.put(pg)
            yield
            Dst = [tppool.get() for _ in range(4)]; DT = [tppool.get() for _ in range(4)]
            for u in range(4):
                us = slice(u * 128, (u + 1) * 128)
                S.act(Dst[u].ap, argL.ap[:, us], AF.Exp, scale=-1.0, bias=colT[u].ap[:, 8 + h:9 + h],
                      reads=[argL, colT[u]], writes=[Dst[u]])
                S.act(DT[u].ap, argU.ap[:, us], AF.Exp, scale=1.0, bias=colT[u].ap[:, 16 + h:17 + h],
                      reads=[argU, colT[u]], writes=[DT[u]])
                yield
            ppool.put(argL, argU)
            rq = rstd_from_chunks([(qs.ap, qs)], 1.0)
            stt(qs.ap, qs.ap, 128.0 ** -0.5, rq.ap, MUL, MUL, [qs, rq], [qs])
            ppool.put(rq)
            yield
            rk = rstd_from_chunks([(ks.ap, ks)], 1.0)
            tt(ks.ap, ks.ap, rk.ap, MUL, [ks, rk], [ks], eng="pool")
            ppool.put(rk)
            yield
            gq = ppool.get()
            tt(gq.ap, qs.ap, gam.ap, MUL, [qs, gam], [gq], eng="pool")
            out.extend([gam, Dst, DT, gq])

        def nxt_gen(h, sil, out):
            yield from front_gen(h, sil)
            yield from mid_gen(h, sil, out)

        sil_next = []; mid_next = []
        run_gens([nxt_gen(0, sil_next, mid_next)])
        for h in range(NH):
            ext = qkvext[0]
            qs, ks, vs, zs = sil_next
            gam, Dst, DT, gq = mid_next
            sil_next = []; mid_next = []
            valkc = [None] * 4; kcdT = [None] * 4; attT = [None] * 4

            def unit_gen(u):
                us = slice(u * 128, (u + 1) * 128)
                ct = colT[u]
                pA = pq.get()
                S.mm(pA.ap, ks.ap[:, us], ks.ap[:, us], True, True, [ks], [pA])
                Lt = trpool.get()
                stt(Lt.ap, pA.ap, ct.ap[:, h:h + 1], Dst[u].ap, MUL, MUL, [pA, ct, Dst[u]], [Lt])
                pq.put(pA)
                yield
                pT = pq.get()
                S.mm(pT.ap, ks.ap[:, us], qs.ap[:, us], True, True, [ks, qs], [pT])
                at = tppool.get()
                tt(at.ap, pT.ap, DT[u].ap, MUL, [pT, DT[u]], [at])
                pq.put(pT)
                attT[u] = at
                yield
                cur = r2pool.get()
                pk = pq.get()
                S.tr(pk.ap, ks.ap[:, us], ident.ap, [ks, ident], [pk])
                ts(cur.ap[:, 128:256], pk.ap, ct.ap[:, 24 + h:25 + h], MUL, [pk, ct], [cur])
                ts(KdA[u].ap[0:64, :], pk.ap[0:64, :], ct.ap[0:64, 32 + h:33 + h], MUL, [pk, ct], [KdA[u]])
                ts(KdB[u].ap[64:128, :], pk.ap[64:128, :], ct.ap[64:128, 32 + h:33 + h], MUL, [pk, ct], [KdB[u]])
                pq.put(pk)
                yield
                pv = pq.get()
                S.tr(pv.ap, vs.ap[:, us], ident.ap, [vs, ident], [pv])
                ts(cur.ap[:, 0:128], pv.ap, ct.ap[:, h:h + 1], MUL, [pv, ct], [cur])
                pq.put(pv)
                yield
                pl = pq.get()
                S.tr(pl.ap, Lt.ap, ident.ap, [Lt, ident], [pl])
                LTf = trpool.get()
                S.act(LTf.ap, pl.ap, AF.Copy, reads=[pl], writes=[LTf])
                pq.put(pl)
                yield
                Ld = trpool.get(); LdT = trpool.get(); BT = trpool.get()
                tt(Ld.ap, Lt.ap, M32.ap, MUL, [Lt, M32], [Ld], eng="pool")
                tt(LdT.ap, LTf.ap, M32.ap, MUL, [LTf, M32], [LdT], eng="pool")
                tt(BT.ap, LTf.ap, LdT.ap, SUB, [LTf, LdT], [BT], eng="pool")
                trpool.put(Lt, LTf)
                yield
                Mt = trpool.get()
                tt(Mt.ap, ident.ap, LdT.ap, SUB, [ident, LdT], [Mt], eng="pool")
                P, PT = Ld, LdT
                for lev in range(4):
                    p2_ = pq.get()
                    S.mm(p2_.ap, PT.ap, P.ap, True, True, [P, PT], [p2_])
                    P2 = trpool.get()
                    S.dve(lambda e, P2=P2, p2_=p2_: e.tensor_copy(out=P2.ap, in_=p2_.ap), [p2_], [P2])
                    pq.put(p2_)
                    PT2 = None
                    if lev < 3:
                        pT2 = pq.get()
                        S.mm(pT2.ap, P.ap, PT.ap, True, True, [P, PT], [pT2])
                        PT2 = trpool.get()
                        S.act(PT2.ap, pT2.ap, AF.Copy, reads=[pT2], writes=[PT2])
                        pq.put(pT2)
                    trpool.put(P, PT)
                    yield
                    pm_ = pq.get()
                    S.mm(pm_.ap, P2.ap, Mt.ap, True, True, [P2, Mt], [pm_])
                    tt(Mt.ap, Mt.ap, pm_.ap, ADD, [Mt, pm_], [Mt])
                    pq.put(pm_)
                    P, PT = P2, PT2
                    yield
                trpool.put(P)
                ph = phalf.get()
                S.mm(ph.ap, Mt.ap, cur.ap, True, True, [Mt, cur], [ph])
                Y_ = r2pool.get()
                S.act(Y_.ap, ph.ap, AF.Copy, reads=[ph], writes=[Y_])
                phalf.put(ph); r2pool.put(cur)
                yield
                ph = phalf.get()
                S.mm(ph.ap, BT.ap, Y_.ap, True, True, [BT, Y_], [ph])
                Z_ = r2pool.get()
                S.act(Z_.ap, ph.ap, AF.Copy, reads=[ph], writes=[Z_])
                phalf.put(ph)
                yield
                ph = phalf.get()
                S.mm(ph.ap, Mt.ap, Z_.ap, True, True, [Mt, Z_], [ph])
                cur = r2pool.get()
                tt(cur.ap, Y_.ap, ph.ap, SUB, [Y_, ph], [cur])
                phalf.put(ph); r2pool.put(Y_, Z_)
                trpool.put(BT, Mt)
                yield
                pk2 = pq.get()
                S.tr(pk2.ap, cur.ap[:, 128:256], ident.ap, [cur, ident], [pk2])
                kt = tppool.get()
                S.act(kt.ap, pk2.ap, AF.Copy, reads=[pk2], writes=[kt])
                pq.put(pk2)
                kcdT[u] = kt; valkc[u] = cur

            for grp in ((0, 1, 2, 3),):
                gens = [unit_gen(u) for u in grp]
                while gens:
                    for g_ in list(gens):
                        try:
                            next(g_)
                        except StopIteration:
                            gens.remove(g_)
            tppool.put(*Dst); tppool.put(*DT)
            ppool.put(qs, ks, vs)
            po = pfull.get()

            def scan_gen():
                for u in range(4):
                    for hf in range(2):
                        r0_ = hf * 64
                        cs = slice(u * 128 + r0_, u * 128 + r0_ + 64)
                        pP = pq.get()
                        S.mm(pP.ap, kcdT[u].ap, Sst[h].ap, True, True, [kcdT[u], Sst[h]], [pP])
                        S.mm(po.ap[:, cs], Sst[h].ap, gq.ap[:, cs], True, False, [Sst[h], gq], [po])
                        tt(vnew.ap[r0_:r0_ + 64, :], valkc[u].ap[r0_:r0_ + 64, 0:128], pP.ap[r0_:r0_ + 64, :], SUB,
                           [valkc[u], pP], [vnew])
                        pq.put(pP)
                        pS = pq.get()
                        Kd = KdA[u] if hf == 0 else KdB[u]
                        S.mm(pS.ap, Kd.ap, vnew.ap, True, True, [Kd, vnew], [pS])
                        S.mm(po.ap[:, cs], vnew.ap, attT[u].ap[:, r0_:r0_ + 64], False, True, [vnew, attT[u]], [po])
                        gl = u * 128 + r0_ + 63
                        stt(Sst[h].ap, Sst[h].ap, gam.ap[:, gl:gl + 1], pS.ap, MUL, ADD, [Sst[h], gam, pS], [Sst[h]])
                        pq.put(pS)
                        yield
            sg_ = scan_gen()
            gl_ = [sg_, sg_]
            if h + 1 < NH:
                gl_.append(nxt_gen(h + 1, sil_next, mid_next))
            if b == 0:
                gl_.append(modB)
            run_gens(gl_)
            for u in range(4):
                r2pool.put(valkc[u]); tppool.put(kcdT[u], attT[u])
            ppool.put(gq, gam)
            osb = ppool.get()
            S.act(osb.ap, po.ap, AF.Copy, reads=[po], writes=[osb])
            pfull.put(po)
            ro = rstd_from_chunks([(osb.ap, osb)], 128.0)
            stt(osb.ap, osb.ap, vcol("gnw"), ro.ap, MUL, MUL, [osb, vecT, ro], [osb])
            tt(oT[h].r, osb.ap, zs.ap, MUL, [osb, zs], [oT[h]])
            ppool.put(osb, ro, zs)
            if last:
                S.dma("pool", ndelta_p[h], Sst[h].ap, [Sst[h]], [], out_final=True)
        ppool.put(Grow)
        if b == 0:
            finish_modB()
        if smp:
            exq = [ppool.get() for _ in range(3)]
            exqv = [u_.ap.rearrange("p (c j s) -> p c j s", j=4, s=NS) for u_ in exq]
            for g_ in range(6):
                stq = ppool.get()
                S.dve(lambda e, stq=stq: e.memset(stq.ap, 0.0), [], [stq])
                S.dma("pool", stq.ap[0:48, :], st_qkv[:, :, g_ * 512:(g_ + 1) * 512].rearrange("s j c -> (s j) c"), [], [stq])
                for cc in range(4):
                    ch = g_ * 4 + cc
                    t_ = pq.get()
                    S.tr(t_.ap, stq.ap[:, cc * 128:(cc + 1) * 128], ident.ap, [stq, ident], [t_])
                    S.act(exqv[ch // 8][:, ch % 8, 0:3, :], t_.ap[:, 0:48].rearrange("p (s j) -> p j s", j=3),
                          AF.Copy, reads=[t_], writes=[exq[ch // 8]])
                    pq.put(t_)
                ppool.put(stq)
            S.dma("pool", nqkv_s[:, 0:2, :], st_qkv[:, 1:3, :], [], [], out_final=True)
            s_to_tm(lambda c: qkvS.ap[:, c, :], 24, lambda g0, ng: nqkv_s[:, 2, g0 * 128:(g0 + ng) * 128], [qkvS])
            for ch in range(24):
                ev = exqv[ch // 8]
                S.act(ev[:, ch % 8, 3, :], qkvS.ap[:, ch, :], AF.Copy, reads=[qkvS], writes=[exq[ch // 8]])
                ts(qkvC.ap[:, ch, :], ev[:, ch % 8, 0, :], gcT.ap[:, 0, ch:ch + 1], MUL, [exq[ch // 8], gcT], [qkvC])
                for j in range(1, 4):
                    stt(qkvC.ap[:, ch, :], ev[:, ch % 8, j, :], gcT.ap[:, j, ch:ch + 1], qkvC.ap[:, ch, :], MUL, ADD,
                        [exq[ch // 8], gcT, qkvC], [qkvC])
            ppool.put(*exq)
            th_ = ppool.get()
            ts(flat(qkvC), flat(qkvC), 0.5, MUL, [qkvC], [qkvC])
            S.act(th_.ap[:, 0:24 * NS], flat(qkvC), AF.Tanh, reads=[qkvC], writes=[th_])
            stt(flat(qkvC), th_.ap[:, 0:24 * NS], 1.0, flat(qkvC), ADD, MUL, [th_, qkvC], [qkvC])
            ppool.put(th_)
            qf, kf, vf = flat(qkvC, 0, 8), flat(qkvC, 8, 16), flat(qkvC, 16, 24)

            def bc_sum(src_ap, reads_):
                t_ = pq.get()
                S.mm(t_.ap, ones.ap, src_ap, True, True, [ones] + reads_, [t_])
                o_ = tppool.get()
                S.act(o_.ap, t_.ap, AF.Copy, reads=[t_], writes=[o_])
                pq.put(t_)
                return o_
            tmp_ = tppool.get()
            for f_, scl in ((qf, 128.0 ** -0.5), (kf, 1.0)):
                tt(tmp_.ap, f_, f_, MUL, [qkvC], [tmp_])
                ss_ = bc_sum(tmp_.ap, [tmp_])
                S.act(ss_.ap, ss_.ap, AF.Ln, bias=epsb.ap, reads=[ss_, epsb], writes=[ss_])
                S.act(ss_.ap, ss_.ap, AF.Exp, scale=-0.5, reads=[ss_], writes=[ss_])
                stt(f_, f_, scl, ss_.ap, MUL, MUL, [qkvC, ss_], [qkvC])
                tppool.put(ss_)
            E_ = tppool.get()
            bcs = []
            for row in range(2):
                S.dve(lambda e, E_=E_: e.memset(E_.ap, 0.0), [], [E_])
                for h in range(NH):
                    ts(E_.ap[0:8, h * NS:(h + 1) * NS], ba8.ap[:, row, :], ident.ap[0:8, h:h + 1], MUL, [ba8, ident], [E_])
                bcs.append(bc_sum(E_.ap, [E_]))
            tppool.put(E_)
            bbc, gbc = bcs
            S.act(gbc.ap, gbc.ap, AF.Exp, reads=[gbc], writes=[gbc])
            Wq = tppool.get(); Wk = tppool.get(); bv = tppool.get(); vnT = tppool.get()
            tt(Wq.ap, qf, gbc.ap, MUL, [qkvC, gbc], [Wq])
            tt(Wk.ap, kf, gbc.ap, MUL, [qkvC, gbc], [Wk])
            stt(Wk.ap, Wk.ap, -1.0, bbc.ap, MUL, MUL, [Wk, bbc], [Wk])
            tt(bv.ap, vf, bbc.ap, MUL, [qkvC, bbc], [bv])
            O1 = pq.get(); O2 = pq.get()
            grp_l = [(h, g_) for h in range(NH) for g_ in range(4)]
            loaded = {}

            def load_state(i_):
                h_, g2 = grp_l[i_]
                u_ = ppool.get()
                S.dma("sp", u_.ap.rearrange("p (s v) -> p s v", v=128),
                      st_delta[4 * g2:4 * g2 + 4, h_].rearrange("s k v -> k s v"), [], [u_])
                loaded[i_] = u_
            PF = 3
            for i_ in range(PF):
                load_state(i_)
            for gi, (h, g_) in enumerate(grp_l):
                if True:
                    if gi + PF < len(grp_l):
                        load_state(gi + PF)
                    S4 = loaded.pop(gi)
                    S4v = S4.ap.rearrange("p (s v) -> p s v", v=128)
                    for j in range(4):
                        p_ = h * NS + 4 * g_ + j
                        S.mm(O1.ap[:, p_:p_ + 1], S4v[:, j, :], Wq.ap[:, p_:p_ + 1], True, True, [S4, Wq], [O1])
                        S.mm(O2.ap[:, p_:p_ + 1], S4v[:, j, :], Wk.ap[:, p_:p_ + 1], True, True, [S4, Wk], [O2])
                    c0_ = h * NS + 4 * g_
                    tt(vnT.ap[:, c0_:c0_ + 4], bv.ap[:, c0_:c0_ + 4], O2.ap[:, c0_:c0_ + 4], ADD, [bv, O2], [vnT])
                    Vps = [tppool.get() for _ in range(4)]
                    pvs = [pq.get() for _ in range(4)]
                    for j in range(4):
                        p_ = h * NS + 4 * g_ + j
                        ts(Vps[j].ap, ones.ap, vnT.ap[:, p_:p_ + 1], MUL, [ones, vnT], [Vps[j]])
                    for j in range(4):
                        S.mm(pvs[j].ap, Vps[j].ap, ident.ap, True, True, [Vps[j], ident], [pvs[j]])
                    for j in range(4):
                        p_ = h * NS + 4 * g_ + j
                        ts(Vps[j].ap, pvs[j].ap, kf[:, p_:p_ + 1], MUL, [pvs[j], qkvC], [Vps[j]])
                        stt(S4v[:, j, :], S4v[:, j, :], gbc.ap[:, p_:p_ + 1], Vps[j].ap, MUL, ADD, [S4, gbc, Vps[j]], [S4])
                    pq.put(*pvs); tppool.put(*Vps)
                    S.dma("pool", ndelta_s[4 * g_:4 * g_ + 4, h].rearrange("s k v -> k s v"), S4v, [S4], [], out_final=True)
                    ppool.put(S4)
            tt(tmp_.ap, qf, kf, MUL, [qkvC], [tmp_])
            qk_ = bc_sum(tmp_.ap, [tmp_])
            tt(qk_.ap, qk_.ap, vnT.ap, MUL, [qk_, vnT], [qk_])
            tt(qk_.ap, qk_.ap, O1.ap, ADD, [qk_, O1], [qk_])
            pq.put(O1, O2)
            tt(tmp_.ap, qk_.ap, qk_.ap, MUL, [qk_], [tmp_])
            ss_ = bc_sum(tmp_.ap, [tmp_])
            S.act(ss_.ap, ss_.ap, AF.Ln, scale=1.0 / 128.0, bias=epsb.ap, reads=[ss_, epsb], writes=[ss_])
            S.act(ss_.ap, ss_.ap, AF.Exp, scale=-0.5, reads=[ss_], writes=[ss_])
            stt(qk_.ap, qk_.ap, vcol("gnw"), ss_.ap, MUL, MUL, [qk_, vecT, ss_], [qk_])
            ts(flat(zS), flat(zS), 0.5, MUL, [zS], [zS])
            S.act(tmp_.ap, flat(zS), AF.Tanh, reads=[zS], writes=[tmp_])
            stt(tmp_.ap, tmp_.ap, 1.0, flat(zS), ADD, MUL, [tmp_, zS], [tmp_])
            tt(flat(oTs).bitcast(F32R), qk_.ap, tmp_.ap, MUL, [qk_, tmp_], [oTs])
            tppool.put(tmp_, ss_, qk_, bbc, gbc, Wq, Wk, bv, vnT)
        chk("oT", oT[0].ap, oT[0])

        for e_ in range(8):
            pgb = pfull.get()
            proj(OFF_GB + e_ * 128, hT, pgb.ap, pgb, samp=(hsT, gS.ap[:, e_, :], gS, "tanh") if smp else None)
            tgb = ppool.get()
            S.act(tgb.ap, pgb.ap, AF.Tanh, scale=0.5, reads=[pgb], writes=[tgb])
            pfull.put(pgb)
            pyb = pfull.get()
            proj(e_ * 128, oT, pyb.ap, pyb, wview=w_go_v, samp=(oTs, rawS.ap[:, e_, :], rawS, "copy") if smp else None)
            if smp:
                stt(rawS.ap[:, e_, :], gS.ap[:, e_, :], 1.0, rawS.ap[:, e_, :], ADD, MUL, [gS, rawS], [rawS])
                tt(mgS.r[:, e_, :], mgS.ap[:, e_, :], rawS.ap[:, e_, :], ADD, [mgS, rawS], [mgS])
            stt(tgb.ap, tgb.ap, 1.0, pyb.ap, ADD, MUL, [tgb, pyb], [tgb])
            pfull.put(pyb)
            tt(mg[e_].r, mg[e_].ap, tgb.ap, ADD, [mg[e_], tgb], [mg[e_]])
            ppool.put(tgb)
        rpool.put(*oT)
        for e_ in range(8):
            pat = pfull.get()
            proj(e_ * 128, mg, pat.ap, pat, wview=w_o_v, samp=(mgS, rawS.ap[:, e_, :], rawS, "copy") if smp else None)
            if smp:
                tt(rawS.ap[:, e_, :], rawS.ap[:, e_, :], modS.ap[:, 16 + e_, :], MUL, [rawS, modS], [rawS])
                stt(xsT.ap[:, e_, :], rawS.ap[:, e_, :], 0.5, xsT.ap[:, e_, :], MUL, ADD, [rawS, xsT], [xsT])
            stt(xT[e_].ap, pat.ap, HG1p(e_), xT[e_].ap, MUL, ADD, [pat, dp, xT[e_]], [xT[e_]])
            pfull.put(pat)
        rpool.put(*mg)
        chk("x1", xT[0].ap, xT[0])

        rs = rstd_from_chunks([(xT[c].ap, xT[c]) for c in range(8)], float(D))
        for c in range(8):
            tt(hT[c].r, xT[c].ap, rs.ap, MUL, [xT[c], rs], [hT[c]])
            ts(hT[c].r, hT[c].ap, A2p(c), MUL, [hT[c], dp], [hT[c]])
            ts(hT[c].r, hT[c].ap, B2p(c), ADD, [hT[c], modP], [hT[c]])
        ppool.put(rs)
        if smp:
            s_modnorm(hsT, "n2", 32, 24)
        for half in range(2):
            fT = [rpool.get() for _ in range(16)]
            for fc in range(16):
                pf = pfull.get()
                proj((half * 16 + fc) * 128, hT, pf.ap, pf, wview=w_f1_v,
                     samp=(hsT, sA.ap[:, fc % 8, :], sA, "relu") if smp else None)
                if smp:
                    tt(fS.r[:, half * 16 + fc, :], sA.ap[:, fc % 8, :], sA.ap[:, fc % 8, :], MUL, [sA], [fS])
                r_ = ppool.get()
                S.act(r_.ap, pf.ap, AF.Relu, reads=[pf], writes=[r_])
                pfull.put(pf)
                S.pool(lambda e, fT=fT, fc=fc, r_=r_: e.tensor_tensor(out=fT[fc].r, in0=r_.ap, in1=r_.ap, op=MUL),
                       [r_], [fT[fc]])
                ppool.put(r_)
            for e_ in range(8):
                pf2 = pfull.get()
                for g_ in range(2):
                    wt = load_w(w_f2_v[:, half * 16 + g_ * 8:half * 16 + g_ * 8 + 8, e_ * 128:(e_ + 1) * 128])
                    for kc in range(8):
                        S.mm(pf2.ap, wt.r[:, kc, :], fT[g_ * 8 + kc].r, g_ == 0 and kc == 0, g_ == 1 and kc == 7,
                             [wt, fT[g_ * 8 + kc]], [pf2])
                    if smp:
                        if g_ == 0:
                            ts_ = pq.get()
                        for kc in range(8):
                            S.mm(ts_.ap[:, 0:NS], wt.r[:, kc, :], fS.r[:, half * 16 + g_ * 8 + kc, :],
                                 g_ == 0 and kc == 0, g_ == 1 and kc == 7, [wt, fS], [ts_])
                        if g_ == 1:
                            tt(rawS.ap[:, e_, :], ts_.ap[:, 0:NS], modS.ap[:, 40 + e_, :], MUL, [ts_, modS], [rawS])
                            tt(xsT.ap[:, e_, :], xsT.ap[:, e_, :], rawS.ap[:, e_, :], ADD, [xsT, rawS], [xsT])
                            pq.put(ts_)
                    wpool.put(wt)
                stt(xT[e_].ap, pf2.ap, G2p(e_), xT[e_].ap, MUL, ADD, [pf2, modP, xT[e_]], [xT[e_]])
                pfull.put(pf2)
            rpool.put(*fT)
        rpool.put(*hT)
        chk("x2", xT[0].ap, xT[0])

        rs = rstd_from_chunks([(xT[c].ap, xT[c]) for c in range(8)], float(D))
        for c in range(8):
            stt(xT[c].ap, xT[c].ap, vcol("nf", c), rs.ap, MUL, MUL, [xT[c], vecT, rs], [xT[c]])
        ppool.put(rs)
        for s_ in range(4):
            for hf in range(2):
                pb = pfull.get()
                for cc in range(4):
                    c = hf * 4 + cc
                    S.tr(pb.ap[:, cc * 128:(cc + 1) * 128], xT[c].ap[:, s_ * 128:(s_ + 1) * 128], ident.ap,
                         [xT[c], ident], [pb])
                o_ = ppool.get()
                if hf == 0:
                    S.act(o_.ap, pb.ap, AF.Copy, reads=[pb], writes=[o_])
                else:
                    S.dve(lambda e, o_=o_, pb=pb: e.tensor_copy(out=o_.ap, in_=pb.ap), [pb], [o_])
                pfull.put(pb)
                S.dma("pool", y_p[t0 + s_ * 128:t0 + (s_ + 1) * 128, hf * 512:(hf + 1) * 512], o_.ap, [o_], [],
                      out_final=True)
                ppool.put(o_)
        ppool.put(*xT)
        if smp:
            rs_ = s_rstd(xsT, 8, float(D))
            for kc in range(8):
                ts(rawS.ap[:, kc, :], xsT.ap[:, kc, :], vcol("nf", kc), MUL, [xsT, vecT], [rawS])
                tt(rawS.ap[:, kc, :], rawS.ap[:, kc, :], rs_.ap[:, 0:NS], MUL, [rawS, rs_], [rawS])
            tppool.put(rs_)
            s_to_tm(lambda c: rawS.ap[:, c, :], 8, lambda g0, ng: y_s[:, g0 * 128:(g0 + ng) * 128], [rawS])


_CACHE = {}


def _prep_inputs(inputs):
    f = lambda a: np.ascontiguousarray(np.asarray(a, dtype=np.float32))
    g = {k: f(v) for k, v in inputs.items()}
    shared = {
        "w_ada": g["w_ada"][0], "b_ada": g["b_ada"][0], "norm1_w": g["norm1_w"][0], "w_in": g["w_in"][0],
        "conf_dw_w": g["conf_dw_w"][0], "conf_dw_b": g["conf_dw_b"][0], "conf_ln_w": g["conf_ln_w"][0],
        "conf_ln_b": g["conf_ln_b"][0], "w_conf_out": g["w_conf_out"][0], "gdn_conv_w": g["gdn_conv_w"][0],
        "a_log": g["a_log"][0], "dt_bias": g["dt_bias"][0], "gdn_norm_w": g["gdn_norm_w"][0],
        "w_gdn_out": g["w_gdn_out"][0], "w_o": g["w_o"][0], "norm2_w": g["norm2_w"][0], "w_ff1": g["w_ff1"][0],
        "w_ff2": g["w_ff2"][0], "final_norm_w": g["final_norm_w"],
    }
    for nm in ("w_ada", "w_in", "w_conf_out", "w_gdn_out", "w_o", "w_ff1", "w_ff2"):
        shared[nm] = np.ascontiguousarray(np.pad(shared[nm], ((0, 0), (0, 128))))
    in_maps = []
    for c in range(8):
        sl = slice(c * NS, (c + 1) * NS)
        m = dict(shared)
        m["x_p"] = g["x_prompt"][c]
        m["x_s"] = np.ascontiguousarray(g["x_sample"][sl, 0, :])
        m["c_all"] = np.ascontiguousarray(np.concatenate([g["c_prompt"][c:c + 1], g["c_sample"][sl]], axis=0))
        m["st_conf"] = np.ascontiguousarray(g["state_conf_conv"][0, sl])
        m["st_qkv"] = np.ascontiguousarray(g["state_qkv_conv"][0, sl])
        m["st_delta"] = np.ascontiguousarray(g["state_delta"][0, sl])
        in_maps.append(m)
    return in_maps


def kernel(**inputs):
    if "nc" not in _CACHE:
        _CACHE["nc"] = build_program()
    nc = _CACHE["nc"]
    in_maps = _prep_inputs(inputs)
    res = run_bass_kernel_spmd(nc, in_maps, core_ids=list(range(8)))
    r = res.results
    y_prompt = np.stack([r[c]["y_p"] for c in range(8)], axis=0)
    y_sample = np.concatenate([r[c]["y_s"] for c in range(8)], axis=0)[:, None, :]
    nconf_p = np.stack([r[c]["nconf_p"] for c in range(8)], axis=0)[None]
    nqkv_p = np.stack([r[c]["nqkv_p"] for c in range(8)], axis=0)[None]
    ndelta_p = np.stack([r[c]["ndelta_p"] for c in range(8)], axis=0)[None]
    nconf_s = np.concatenate([r[c]["nconf_s"] for c in range(8)], axis=0)[None]
    nqkv_s = np.concatenate([r[c]["nqkv_s"] for c in range(8)], axis=0)[None]
    ndelta_s = np.concatenate([r[c]["ndelta_s"] for c in range(8)], axis=0)[None]
    return tuple(np.ascontiguousarray(a, dtype=np.float32) for a in
                 (y_prompt, y_sample, nconf_p, nqkv_p, ndelta_p, nconf_s, nqkv_s, ndelta_s))
```
